# Optimizing a Trainium2 kernel written in Bass

```python
import jax, jax.numpy as jnp
from jax import lax
import numpy as np

D_MODEL = 1024
BATCH = 4
SEQ = 8192
DEPTH = 2
DEC_BATCH = 128
DEC_SEQ = 4
PAST_LEN = 16384
PAGE_SIZE = 128

N_A_LAYERS = DEPTH // 2
N_B_LAYERS = DEPTH - N_A_LAYERS
CHUNK = 128
GM_HALF = 2 * D_MODEL
GM_GROUPS = 8
GM_GROUP_DIM = GM_HALF // GM_GROUPS
HEAD_DIM = 64
N_HEADS = D_MODEL // HEAD_DIM
N_KV_HEADS = N_HEADS // 8
Q_PER_KV = N_HEADS // N_KV_HEADS
WINDOW = 128
ROT_DIM = HEAD_DIM // 4
ROPE_THETA = 500000.0
D_FF = 2816
CONV_W = 3
PLE_DIM = 256
EPS = 1e-6
NEG_INF = -1e30

kernel_name = "yoco_gmlp_swa_sink_convffn_step"


def rmsnorm(x, w):
    xf = x.astype(jnp.float32)
    y = xf * lax.rsqrt(jnp.mean(xf * xf, axis=-1, keepdims=True) + EPS)
    return (y * w.astype(jnp.float32)).astype(x.dtype)


def partial_rope(x, pos):
    half = ROT_DIM // 2
    inv_freq = ROPE_THETA ** (-jnp.arange(half, dtype=jnp.float32) / half)
    ang = pos.astype(jnp.float32)[:, None] * inv_freq[None, :]
    cos = jnp.cos(ang)[None, :, None, :].astype(x.dtype)
    sin = jnp.sin(ang)[None, :, None, :].astype(x.dtype)
    x1 = x[..., :half]
    x2 = x[..., half:ROT_DIM]
    return jnp.concatenate([x1 * cos - x2 * sin, x2 * cos + x1 * sin, x[..., ROT_DIM:]], axis=-1)


def chunk_gmlp(h, w_in, v_norm, w_s, b_s, w_out, chunk_len):
    B, T, _ = h.shape
    z = jax.nn.gelu(h @ w_in)
    u, v = z[..., :GM_HALF], z[..., GM_HALF:]
    v = rmsnorm(v, v_norm)
    n = T // chunk_len
    L = chunk_len
    causal = jnp.tril(jnp.ones((L, L), dtype=bool))
    ws = jnp.where(causal[None], w_s[:, :L, :L], 0).astype(v.dtype)
    vb = v.reshape(B, n, L, GM_GROUPS, GM_GROUP_DIM)
    mixed = jnp.einsum('gts,bnsgc->bntgc', ws, vb) + b_s[:, :L].T[None, None, :, :, None].astype(v.dtype)
    out = u * mixed.reshape(B, T, GM_HALF)
    return out @ w_out, v


def conv_ffn(h, conv_state, w_gate, w_up, conv_w, conv_b, w_down):
    T = h.shape[1]
    g = h @ w_gate
    gp = jnp.concatenate([conv_state.astype(g.dtype), g], axis=1)
    conv = conv_b.astype(g.dtype)
    for k in range(CONV_W):
        conv = conv + gp[:, k:k + T] * conv_w[k]
    act = jax.nn.gelu(conv) * (h @ w_up)
    return act @ w_down, gp[:, T:]


def band_attention(q, k, v, sinks, k_past, v_past):
    B, T = q.shape[0], q.shape[1]
    if k_past is None:
        nb = T // WINDOW
        lq = WINDOW
        pad = jnp.zeros((B, WINDOW, N_KV_HEADS, HEAD_DIM), k.dtype)
        kp = jnp.concatenate([pad, k], axis=1).reshape(B, nb + 1, WINDOW, N_KV_HEADS, HEAD_DIM)
        vp = jnp.concatenate([pad, v], axis=1).reshape(B, nb + 1, WINDOW, N_KV_HEADS, HEAD_DIM)
        kb = jnp.concatenate([kp[:, :-1], kp[:, 1:]], axis=2)
        vb = jnp.concatenate([vp[:, :-1], vp[:, 1:]], axis=2)
        block_ok = (jnp.arange(nb)[:, None, None] > 0) | (jnp.arange(2 * WINDOW)[None, None, :] >= WINDOW)
    else:
        nb = 1
        lq = T
        kb = jnp.concatenate([k_past.astype(k.dtype), k], axis=1)[:, None]
        vb = jnp.concatenate([v_past.astype(v.dtype), v], axis=1)[:, None]
        block_ok = jnp.ones((1, 1, WINDOW + T), dtype=bool)
    lk = WINDOW + lq
    qi = jnp.arange(lq)[:, None]
    sj = jnp.arange(lk)[None, :]
    mask = ((sj > qi) & (sj <= qi + WINDOW))[None] & block_ok
    qb = q.reshape(B, nb, lq, N_KV_HEADS, Q_PER_KV, HEAD_DIM)
    s = jnp.einsum('bnqkgd,bnskd->bnkgqs', qb, kb, preferred_element_type=jnp.float32) * (HEAD_DIM ** -0.5)
    s = jnp.where(mask[None, :, None, None], s, NEG_INF)
    sink = jnp.broadcast_to(sinks.astype(jnp.float32).reshape(1, 1, N_KV_HEADS, Q_PER_KV, 1, 1), s.shape[:-1] + (1,))
    p = jax.nn.softmax(jnp.concatenate([s, sink], axis=-1), axis=-1)[..., :-1].astype(vb.dtype)
    o = jnp.einsum('bnkgqs,bnskd->bnqkgd', p, vb)
    return o.reshape(B, T, N_HEADS * HEAD_DIM)


def setup_inputs(seed: int = 0) -> dict:
    key = jax.random.key(seed)
    ks = jax.random.split(key, 32)

    def nrm(k, shape, scale=1.0):
        return jax.random.normal(k, shape, jnp.float32) * scale

    def gain(k, shape):
        return 1.0 + 0.05 * jax.random.normal(k, shape, jnp.float32)

    kv_dim = N_KV_HEADS * HEAD_DIM
    return {
        "x_prompt": nrm(ks[0], (BATCH, SEQ, D_MODEL)),
        "x_sample": nrm(ks[1], (DEC_BATCH, DEC_SEQ, D_MODEL)),
        "state_ffn_conv": nrm(ks[2], (DEPTH, DEC_BATCH, CONV_W - 1, D_FF)),
        "cache_k_win": nrm(ks[3], (DEC_BATCH, WINDOW, N_KV_HEADS, HEAD_DIM)),
        "cache_v_win": nrm(ks[4], (DEC_BATCH, WINDOW, N_KV_HEADS, HEAD_DIM)),
        "p_prompt": nrm(ks[5], (DEPTH, BATCH, SEQ, PLE_DIM)),
        "p_sample": nrm(ks[6], (DEPTH, DEC_BATCH, DEC_SEQ, PLE_DIM)),
        "norm_mix": gain(ks[7], (DEPTH, D_MODEL)),
        "gm_w_in": nrm(ks[8], (N_A_LAYERS, D_MODEL, 2 * GM_HALF), D_MODEL ** -0.5),
        "gm_v_norm": gain(ks[9], (N_A_LAYERS, GM_HALF)),
        "gm_w_s": nrm(ks[10], (N_A_LAYERS, GM_GROUPS, CHUNK, CHUNK), CHUNK ** -0.5),
        "gm_b_s": 1.0 + 0.1 * nrm(ks[11], (N_A_LAYERS, GM_GROUPS, CHUNK)),
        "gm_w_out": nrm(ks[12], (N_A_LAYERS, GM_HALF, D_MODEL), GM_HALF ** -0.5),
        "kv_norm": gain(ks[13], (D_MODEL,)),
        "w_kv": nrm(ks[14], (D_MODEL, 2 * kv_dim), D_MODEL ** -0.5),
        "w_q": nrm(ks[15], (N_B_LAYERS, D_MODEL, N_HEADS * HEAD_DIM), D_MODEL ** -0.5),
        "attn_sinks": nrm(ks[16], (N_B_LAYERS, N_HEADS), 0.5),
        "w_o": nrm(ks[17], (N_B_LAYERS, N_HEADS * HEAD_DIM, D_MODEL), (N_HEADS * HEAD_DIM) ** -0.5),
        "norm_ffn": gain(ks[18], (DEPTH, D_MODEL)),
        "ffn_w_gate": nrm(ks[19], (DEPTH, D_MODEL, D_FF), D_MODEL ** -0.5),
        "ffn_w_up": nrm(ks[20], (DEPTH, D_MODEL, D_FF), D_MODEL ** -0.5),
        "ffn_conv_w": nrm(ks[21], (DEPTH, CONV_W, D_FF), CONV_W ** -0.5),
        "ffn_conv_b": nrm(ks[22], (DEPTH, D_FF), 0.02),
        "ffn_w_down": nrm(ks[23], (DEPTH, D_FF, D_MODEL), D_FF ** -0.5),
        "ple_norm": gain(ks[24], (DEPTH, D_MODEL)),
        "ple_w_gate": nrm(ks[25], (DEPTH, D_MODEL, D_MODEL), D_MODEL ** -0.5),
        "ple_w_proj": nrm(ks[26], (DEPTH, PLE_DIM, D_MODEL), PLE_DIM ** -0.5),
        "final_norm": gain(ks[27], (D_MODEL,)),
    }


def reference(x_prompt, x_sample, state_ffn_conv, cache_k_win, cache_v_win, p_prompt, p_sample,
              norm_mix, gm_w_in, gm_v_norm, gm_w_s, gm_b_s, gm_w_out, kv_norm, w_kv, w_q, attn_sinks, w_o,
              norm_ffn, ffn_w_gate, ffn_w_up, ffn_conv_w, ffn_conv_b, ffn_w_down,
              ple_norm, ple_w_gate, ple_w_proj, final_norm):
    kv_dim = N_KV_HEADS * HEAD_DIM

    def shared_kv(x, pos):
        B, T, _ = x.shape
        kv = rmsnorm(x, kv_norm) @ w_kv
        k = partial_rope(kv[..., :kv_dim].reshape(B, T, N_KV_HEADS, HEAD_DIM), pos)
        v = kv[..., kv_dim:].reshape(B, T, N_KV_HEADS, HEAD_DIM)
        return k, v

    def trunk(x, p, conv_state, k_past, v_past, pos, chunk_len):
        B, T, _ = x.shape
        gm_v, new_conv = [], []
        k = v = None
        for i in range(DEPTH):
            h = rmsnorm(x, norm_mix[i])
            if i < N_A_LAYERS:
                mix, v_rows = chunk_gmlp(h, gm_w_in[i], gm_v_norm[i], gm_w_s[i], gm_b_s[i], gm_w_out[i], chunk_len)
                gm_v.append(v_rows)
            else:
                j = i - N_A_LAYERS
                q = partial_rope((h @ w_q[j]).reshape(B, T, N_HEADS, HEAD_DIM), pos)
                mix = band_attention(q, k, v, attn_sinks[j], k_past, v_past) @ w_o[j]
            x = x + mix
            h = rmsnorm(x, norm_ffn[i])
            ffn, cs = conv_ffn(h, conv_state[i], ffn_w_gate[i], ffn_w_up[i], ffn_conv_w[i], ffn_conv_b[i], ffn_w_down[i])
            new_conv.append(cs)
            x = x + ffn
            gate = jax.nn.sigmoid(rmsnorm(x, ple_norm[i]) @ ple_w_gate[i])
            x = x + (p[i] @ ple_w_proj[i]) * gate
            if i == N_A_LAYERS - 1:
                k, v = shared_kv(x, pos)
        return rmsnorm(x, final_norm), gm_v, new_conv, k, v

    zero_conv = jnp.zeros((DEPTH, x_prompt.shape[0], CONV_W - 1, D_FF), x_prompt.dtype)
    pos_prompt = jnp.arange(SEQ, dtype=jnp.int32)
    pos_sample = PAST_LEN + jnp.arange(DEC_SEQ, dtype=jnp.int32)

    y_prompt, _, conv_p, k_p, v_p = trunk(x_prompt, p_prompt, zero_conv, None, None, pos_prompt, CHUNK)
    y_sample, gm_s, conv_s, k_s, v_s = trunk(x_sample, p_sample, state_ffn_conv, cache_k_win, cache_v_win,
                                             pos_sample, DEC_SEQ)

    new_gm_v_sample = jnp.stack(gm_s, axis=0)
    new_conv_prompt = jnp.stack(conv_p, axis=0)
    new_conv_sample = jnp.stack(conv_s, axis=0)
    new_k_prompt = k_p[:, -WINDOW:]
    new_v_prompt = v_p[:, -WINDOW:]
    return (y_prompt, y_sample, new_gm_v_sample, new_conv_prompt, new_conv_sample,
            new_k_prompt, new_v_prompt, k_s, v_s)
```

```python
import contextlib
import numpy as np
import concourse.bass as bass
import concourse.mybir as mybir
from concourse.bass_utils import run_bass_kernel_spmd

F32 = mybir.dt.float32
BF16 = mybir.dt.bfloat16
I32 = mybir.dt.int32
AF = mybir.ActivationFunctionType
ALU = mybir.AluOpType
AX = mybir.AxisListType

PE, ACT, DVE, POOL, SP = "pe", "act", "dve", "pool", "sp"
ENGS = (PE, ACT, DVE, POOL, SP)

D = 1024
KC = 8
GH = 2048
FF = 2816
NFC = 22
PLE = 256
NCHUNK_CORE = 36
NT = 9
NEG = -30000.0
EPS = 1e-6
SLOT = 5632
NS = 4


class Op:
    __slots__ = ("eng", "fn", "deps", "is_dma", "signal", "tok", "dma_prev", "cost", "idx", "pos", "fin")

    def __init__(self, eng, fn, is_dma, cost):
        self.eng = eng
        self.fn = fn
        self.deps = []
        self.is_dma = is_dma
        self.signal = False
        self.tok = None
        self.dma_prev = None
        self.cost = cost
        self.idx = 0
        self.pos = 0
        self.fin = None


DEFAULT_COST = {PE: 200.0, ACT: 600.0, DVE: 600.0, POOL: 400.0, SP: 100.0}
SCHED_WINDOW = {PE: 160, ACT: 72, DVE: 72, POOL: 12, SP: 12}


class Prog:
    def __init__(self, nc, n_dma_sems=12, schedule=True):
        self.nc = nc
        self.ops = {e: [] for e in ENGS}
        self.last_w = {}
        self.readers = {}
        self.n_dma_sems = n_dma_sems
        self.n = 0
        self.do_schedule = schedule

    def op(self, eng, fn, reads=(), writes=(), dma=False, cost=None):
        if cost is None:
            cost = 4000.0 if dma else DEFAULT_COST[eng]
        o = Op(eng, fn, dma, cost)
        o.idx = self.n
        self.n += 1
        deps = []
        for k in reads:
            w = self.last_w.get(k)
            if w is not None:
                deps.append(w)
        for k in writes:
            w = self.last_w.get(k)
            if w is not None:
                deps.append(w)
            rs = self.readers.get(k)
            if rs:
                deps.extend(rs)
        seen = set()
        for d in deps:
            if id(d) in seen:
                continue
            seen.add(id(d))
            o.deps.append(d)
        for k in writes:
            self.last_w[k] = o
            self.readers[k] = []
        ws = set(writes)
        for k in reads:
            if k in ws:
                continue
            self.readers.setdefault(k, []).append(o)
        self.ops[eng].append(o)
        return o

    def schedule(self):
        pend = {e: list(self.ops[e]) for e in ENGS}
        head = {e: 0 for e in ENGS}
        done_flag = {}
        tfree = {e: 0.0 for e in ENGS}
        order = {e: [] for e in ENGS}
        remaining = sum(len(v) for v in pend.values())
        while remaining:
            progressed = False
            for e in sorted(ENGS, key=lambda q: tfree[q]):
                lst = pend[e]
                h = head[e]
                while h < len(lst) and lst[h] is None:
                    h += 1
                head[e] = h
                if h >= len(lst):
                    continue
                best = None
                best_start = None
                cnt = 0
                i = h
                W = SCHED_WINDOW[e]
                while i < len(lst) and cnt < W:
                    o = lst[i]
                    if o is not None:
                        cnt += 1
                        ready = 0.0
                        ok = True
                        for d in o.deps:
                            f = d.fin
                            if f is None:
                                ok = False
                                break
                            if f > ready:
                                ready = f
                        if ok:
                            st_ = ready if ready > tfree[e] else tfree[e]
                            if best is None or st_ < best_start - 1e-9:
                                best, best_start, best_i = o, st_, i
                                if st_ <= tfree[e]:
                                    break
                    i += 1
                if best is None:
                    continue
                lst[best_i] = None
                if best.is_dma:
                    tfree[e] = best_start + 60.0
                    best.fin = best_start + best.cost
                else:
                    best.fin = best_start + best.cost
                    tfree[e] = best.fin
                order[e].append(best)
                remaining -= 1
                progressed = True
                break
            assert progressed, "scheduler deadlock"
        self.ops = order
        return max(tfree.values())

    def emit(self, final_waits):
        nc = self.nc
        if self.do_schedule:
            self.est_ns = self.schedule()
        for e in ENGS:
            for i, o in enumerate(self.ops[e]):
                o.pos = i
        for e in ENGS:
            for o in self.ops[e]:
                latest = {}
                nd = []
                for d in o.deps:
                    if d.is_dma:
                        nd.append(d)
                        continue
                    if d.eng == e and e == PE and not o.is_dma:
                        continue
                    cur = latest.get(d.eng)
                    if cur is None or d.pos > cur.pos:
                        latest[d.eng] = d
                nd.extend(latest.values())
                o.deps = nd
                for d in nd:
                    d.signal = True
        with contextlib.ExitStack() as st:
            esem = {e: st.enter_context(nc.semaphore("s_" + e)) for e in ENGS}
            dsem = {}
            for e in (SP, ACT, POOL):
                if any(o.is_dma for o in self.ops[e]):
                    dsem[e] = [st.enter_context(nc.semaphore("d_%s_%d" % (e, i))) for i in range(self.n_dma_sems)]
            for e in ENGS:
                cnt = 0
                dcnt = 0
                uses = [0] * self.n_dma_sems
                for o in self.ops[e]:
                    if o.is_dma:
                        si = dcnt % self.n_dma_sems
                        dcnt += 1
                        prev = uses[si]
                        uses[si] += 16
                        o.tok = (dsem[e][si], uses[si])
                        o.dma_prev = (dsem[e][si], prev)
                    elif o.signal:
                        cnt += 1
                        o.tok = (esem[e], cnt)
            block = st.enter_context(nc.Block())
            engobj = {PE: block.tensor, ACT: block.scalar, DVE: block.vector, POOL: block.gpsimd, SP: block.sync}

            def make(e):
                def body(eng):
                    waited = {}

                    def wait(tok):
                        sem, val = tok
                        if val <= 0:
                            return
                        key = id(sem)
                        if waited.get(key, 0) >= val:
                            return
                        eng.wait_ge(sem, val)
                        waited[key] = val

                    for o in self.ops[e]:
                        for d in o.deps:
                            wait(d.tok)
                        if o.is_dma:
                            wait(o.dma_prev)
                            o.fn(eng).then_inc(o.tok[0], 16)
                        else:
                            ins = o.fn(eng)
                            if o.signal:
                                ins.then_inc(o.tok[0], 1)
                    for fo in final_waits.get(e, ()):
                        wait(fo.tok)
                return body

            for e in ENGS:
                if self.ops[e] or final_waits.get(e):
                    engobj[e](make(e))


def build_program(n_tiles=NT, do_sample=True, stage=99, dbg=None):
    nc = bass.Bass("TRN2", target_bir_lowering=False)

    def din(name, shape, dt=F32):
        return nc.dram_tensor(name, list(shape), dt, kind="ExternalInput").ap()

    def dout(name, shape, dt=F32):
        return nc.dram_tensor(name, list(shape), dt, kind="ExternalOutput").ap()

    def dscr(name, shape, dt=BF16):
        return nc.dram_tensor(name, list(shape), dt, kind="Internal").ap()

    NROW = NCHUNK_CORE * 128
    xp_d = din("xp", [NROW, D])
    pp_d = din("pp", [2, NROW, PLE])
    xs_d = din("xs", [64, D])
    pss_d = din("pss", [2, 64, PLE])
    cst_d = din("cst", [2, 32, FF])
    ck_d = din("ck", [16, 128, 128])
    cv_d = din("cv", [16, 128, 128])
    w_in_d = din("gm_w_in", [D, 2 * GH])
    w_out_d = din("gm_w_out", [GH, D])
    w_kv_d = din("w_kv", [D, 256])
    w_q_d = din("w_q", [D, D])
    w_o_d = din("w_o", [D, D])
    w_gate_d = din("ffn_w_gate", [2, D, FF])
    w_up_d = din("ffn_w_up", [2, D, FF])
    w_down_d = din("ffn_w_down", [2, FF, D])
    w_pg_d = din("ple_w_gate", [2, D, D])
    w_pp_d = din("ple_w_proj", [2, PLE, D])
    nwfm_d = din("nw_fm", [128, 8, KC])
    fnw_d = din("fnw", [1, D])
    vnw_d = din("vnw", [1, GH])
    convw_d = din("convw", [128, 2, NFC, 4])
    wsT_d = din("wsT", [128, 8, 128])
    wsTs_d = din("wsTs", [64, 8, 64])
    bsp_d = din("bsp", [1, 8, 128])
    bss_d = din("bss", [1, 8, 64])
    sinks_d = din("sinks", [1, 16])
    sinkrow_d = din("sinkrow", [64, 1])
    ropep_d = din("rope_p", [NROW, 16])
    ropes_d = din("rope_s", [64, 16])
    ident_d = din("ident", [128, 128])
    maska_d = din("mask_a", [128, 256])
    maskb_d = din("mask_b", [128, 256])
    masks_d = din("mask_s", [64, 132])

    yp_d = dout("yp", [32 * 128, D])
    ys_d = dout("ys", [64, D])
    gmv_d = dout("gmv", [64, GH])
    ncp_d = dout("ncp", [2, 2, FF])
    ncs_d = dout("ncs", [2, 32, FF])
    nkp_d = dout("nkp", [128, 128])
    nvp_d = dout("nvp", [128, 128])
    nks_d = dout("nks", [64, 128])
    nvs_d = dout("nvs", [64, 128])

    s_w_in = dscr("s_w_in", [D, 2 * GH])
    s_w_out = dscr("s_w_out", [GH, D])
    s_w_kv = dscr("s_w_kv", [D, 256])
    s_w_q = dscr("s_w_q", [D, D])
    s_w_o = dscr("s_w_o", [D, D])
    s_w_gate = dscr("s_w_gate", [2, D, FF])
    s_w_up = dscr("s_w_up", [2, D, FF])
    s_w_down = dscr("s_w_down", [2, FF, D])
    s_w_pg = dscr("s_w_pg", [2, D, D])
    s_w_pp = dscr("s_w_pp", [2, PLE, D])
    s_kv = dscr("s_kvs", [64, 256])
    s_o = dscr("s_os", [16, 64, 128])

    P = Prog(nc)
    st = contextlib.ExitStack()

    def sb(name, shape, dt):
        return st.enter_context(nc.sbuf_tensor(name, list(shape), dt))

    ps = st.enter_context(nc.psum_tensor("ps", [128, 8, 512], F32))
    psb = ps[:].bitcast(BF16) if False else None

    wring = sb("wring", [128, NS, SLOT], BF16)
    x = sb("x", [128, 5, D], F32)
    cur_tile = [0]

    def XS(c):
        return (4 * cur_tile[0] + c) % 5

    def XK(c):
        return "x%d" % XS(c)
    hT = sb("hT", [128, KC, 512], BF16)
    big = sb("big", [128, NFC, 512], BF16)
    vn = sb("vn", [128, 4, GH], BF16)
    vnflat = vn[:].rearrange("p c f -> p (c f)")
    qT = vnflat[:, 0:4096].rearrange("p (k t) -> p k t", k=KC)
    qb = vnflat[:, 4096:6144].rearrange("p (a f) -> p a f", a=2)
    Pm = vnflat[:, 6144:8192].rearrange("p (a h k) -> p a h k", a=2, h=4)
    htok = sb("htok", [128, 2, D], BF16)
    cb = sb("cb", [128, 2, 512], F32)
    ge = sb("ge", [128, 2, 512], BF16)
    junk = ge[:].rearrange("p a f -> p (a f)")
    GEK = ["ge0", "ge1"]
    gp = sb("gp", [128, 2, 516], F32)
    ub = sb("ub", [128, 2, 512], BF16)
    pf = sb("pf", [128, 2, PLE], F32)
    pb = sb("pb", [128, 2, PLE], BF16)
    pT = sb("pT", [128, 2, 512], BF16)
    th = sb("th", [128, 2, 512], F32)
    kvf = sb("kvf", [128, 2, 256], F32)
    kb2 = sb("kb2", [128, 2, 256], BF16)
    kTd = sb("kTd", [128, 2, 5 * 128], BF16)
    vtok = sb("vtok", [128, 5, 128], BF16)
    Dg = sb("Dg", [128, 2, 4, 128], BF16)
    PT = sb("PT", [128, 2, 4, 2, 128], BF16)
    oT = sb("oT", [128, 2, KC, 128], BF16)
    rope_t = sb("rope_t", [128, 4, 16], F32)
    rt = sb("rt", [128, 4, 16, 8], F32)
    stat = sb("stat", [128, 64], F32)
    att = sb("att", [128, 2, 6, 4], F32)
    gstp = sb("gstp", [128, 2, NFC, 2], F32)
    gsts = sb("gsts", [128, 2, NFC, 32], F32)
    gnew = sb("gnew", [128, NFC, 32], F32)
    nwfm = sb("nwfm", [128, 8, KC], F32)
    fnw = sb("fnw_sb", [128, D], F32)
    vnw = sb("vnw_sb", [128, GH], F32)
    convw = sb("convw_sb", [128, 2, NFC, 4], F32)
    thflat = th[:].rearrange("p a f -> p (a f)")
    yout = th[:].rearrange("p a f -> p (a f)").unsqueeze(1)
    wsTf = thflat[:, 0:1024].rearrange("p (g t) -> p g t", g=8)
    wsT = sb("wsT_sb", [128, 8, 128], BF16)
    wsTsf = thflat[0:64, 0:512].rearrange("p (g t) -> p g t", g=8)
    wsTs = sb("wsTs_sb", [64, 8, 64], BF16)
    bspf = thflat[0:1, 0:1024].rearrange("p (g t) -> p g t", g=8)
    bsp = sb("bsp_sb", [1, 8, 128], BF16)
    bssf = thflat[0:1, 0:512].rearrange("p (g t) -> p g t", g=8)
    bss = sb("bss_sb", [1, 8, 64], BF16)
    ones = sb("ones", [1, 128], BF16)
    sink8 = sb("sink8", [128, 16], F32)
    sinkr = sb("sinkr", [64, 2], F32)
    identf = sb("identf", [128, 128], F32)
    identb = sb("identb", [128, 128], BF16)
    maskf = thflat[:, 0:256]
    maska = sb("maska", [128, 256], BF16)
    maskb = sb("maskb", [128, 256], BF16)
    masks = sb("masks", [64, 132], BF16)
    nhalf = sb("nhalf", [128, 1], F32)
    bigF = big[:].rearrange("p j t -> p (j t)").bitcast(F32)
    tr_out = bigF[0:32, 0:FF]
    cstf = bigF[0:32, 0:FF]
    ckb = sb("ckb", [128, 16, 128], BF16)
    cvb = sb("cvb", [128, 16, 128], BF16)
    kTc = sb("kTc", [128, 2, 128], BF16)
    knew4 = sb("knew4", [4, 16, 128], BF16)
    kvb = sb("kvb", [64, 256], BF16)
    kTn = sb("kTn", [128, 64], BF16)
    qT2 = sb("qT2", [128, 16, 64], BF16)
    PTs = sb("PTs", [128, 2, 64], BF16)
    PTn = sb("PTn", [4, 2, 64], BF16)
    o_all = sb("o_all", [64, 16, 128], BF16)
    qtmp = sb("qtmp", [128, 512], BF16)
    o_tok = sb("o_tok", [64, D], BF16)

    print('SBUF_REMAIN', nc.sbuf_bytes_remaining)
    bank_ctr = [0]

    def bank(n=1):
        b = bank_ctr[0]
        if n == 2 and b % 2 == 1:
            b = (b + 1) % 8
        bank_ctr[0] = (b + n) % 8
        return b

    stat_ctr = [0]

    def scol(n=1):
        c = stat_ctr[0]
        if c + n > 64:
            c = 0
        stat_ctr[0] = c + n
        return c

    def load(eng, dst, src, key, reads=()):
        return P.op(eng, lambda e: e.dma_start(out=dst, in_=src), reads=list(reads), writes=[key], dma=True)

    TH = ["th0", "th1"]
    load(SP, nwfm[:], nwfm_d, "c_nwfm")
    load(SP, fnw[:], fnw_d.partition_broadcast(128), "c_fnw")
    load(SP, vnw[:], vnw_d.partition_broadcast(128), "c_vnw")
    load(SP, convw[:], convw_d, "c_convw")
    load(SP, sink8[:], sinks_d.partition_broadcast(128), "c_sink8")
    load(SP, sinkr[:, 0:1], sinkrow_d, "c_sinkr")
    load(SP, identf[:], ident_d, "c_identf")
    P.op(DVE, lambda e: e.tensor_copy(out=identb[:], in_=identf[:]), reads=["c_identf"], writes=["c_identb"])
    for (src_d, stg, dstt, kk) in ((wsT_d, wsTf, wsT, "c_wsT"), (wsTs_d, wsTsf, wsTs, "c_wsTs"), (bsp_d, bspf, bsp, "c_bsp"),
                                   (bss_d, bssf, bss, "c_bss"), (maska_d, maskf, maska, "c_maska"), (maskb_d, maskf, maskb, "c_maskb"),
                                   (masks_d, thflat[0:64, 0:132], masks, "c_masks")):
        P.op(SP, lambda e, stg=stg, src_d=src_d: e.dma_start(out=stg, in_=src_d), writes=TH, dma=True)
        P.op(DVE, lambda e, stg=stg, dstt=dstt: e.tensor_copy(out=dstt[:], in_=stg), reads=TH, writes=[kk])
    P.op(DVE, lambda e: e.memset(ones[:], 1.0), writes=["c_ones"])
    P.op(DVE, lambda e: e.memset(nhalf[:], -0.5), writes=["c_nhalf"])
    P.op(DVE, lambda e: e.tensor_scalar(out=sink8[:], in0=sink8[:], scalar1=8.0, scalar2=None, op0=ALU.mult),
         reads=["c_sink8"], writes=["c_sink8"])
    P.op(DVE, lambda e: e.tensor_scalar(out=sinkr[:, 1:2], in0=sinkr[:, 0:1], scalar1=8.0, scalar2=None, op0=ALU.mult),
         reads=["c_sinkr"], writes=["c_sinkr8"])
    P.op(DVE, lambda e: e.memset(gstp[:], 0.0), writes=["gst0", "gst1"])
    P.op(DVE, lambda e: e.memset(kTd[:], 0.0), writes=["kT0", "kT1", "kT2", "kT3", "kT4"])
    P.op(DVE, lambda e: e.memset(vtok[:], 0.0), writes=["vt0", "vt1", "vt2", "vt3", "vt4"])

    cast_list = []
    cast_done = set()

    def cast(dst, src, key):
        cast_list.append((dst, src, key))

    for i in range(4):
        cast(s_w_in[i * 256:(i + 1) * 256, :], w_in_d[i * 256:(i + 1) * 256, :], "S_w_in%d" % i)
    for i in range(2):
        cast(s_w_out[i * 1024:(i + 1) * 1024, :], w_out_d[i * 1024:(i + 1) * 1024, :], "S_w_out%d" % i)
    for l in range(2):
        if l == 1:
            cast(s_w_q, w_q_d, "S_w_q")
            cast(s_w_o, w_o_d, "S_w_o")
        for i in range(2):
            cast(s_w_gate[l, i * 512:(i + 1) * 512, :], w_gate_d[l, i * 512:(i + 1) * 512, :], "S_w_gate%d_%d" % (l, i))
            cast(s_w_up[l, i * 512:(i + 1) * 512, :], w_up_d[l, i * 512:(i + 1) * 512, :], "S_w_up%d_%d" % (l, i))
        for i in range(2):
            cast(s_w_down[l, i * 1408:(i + 1) * 1408, :], w_down_d[l, i * 1408:(i + 1) * 1408, :], "S_w_down%d_%d" % (l, i))
        cast(s_w_pg[l], w_pg_d[l], "S_w_pg%d" % l)
        cast(s_w_pp[l], w_pp_d[l], "S_w_pp%d" % l)
        if l == 0:
            cast(s_w_kv, w_kv_d, "S_w_kv")

    def ensure_cast(keys):
        need = [k for k in keys if k not in cast_done]
        if not need:
            return
        last = max(i for i, (_, _, k) in enumerate(cast_list) if k in need)
        for i in range(last + 1):
            dst, src, k = cast_list[i]
            if k in cast_done:
                continue
            cast_done.add(k)
            P.op(POOL, lambda e, dst=dst, src=src: e.dma_start(out=dst, in_=src), writes=[k], dma=True, cost=40000.0)

    SK = {
        "in": ["S_w_in%d" % i for i in range(4)], "out": ["S_w_out0", "S_w_out1"], "kv": ["S_w_kv"], "q": ["S_w_q"], "o": ["S_w_o"],
        "gate0": ["S_w_gate0_0", "S_w_gate0_1"], "gate1": ["S_w_gate1_0", "S_w_gate1_1"],
        "up0": ["S_w_up0_0", "S_w_up0_1"], "up1": ["S_w_up1_0", "S_w_up1_1"],
        "down0": ["S_w_down0_0", "S_w_down0_1"], "down1": ["S_w_down1_0", "S_w_down1_1"],
        "pg0": ["S_w_pg0"], "pg1": ["S_w_pg1"], "pp0": ["S_w_pp0"], "pp1": ["S_w_pp1"],
    }

    class Ring:
        def __init__(self):
            self.plan = []
            self.planning = True
            self.i = 0
            self.issued = 0
            self.seen = set()

        def _issue(self, j):
            src, kc, ncol, skeys, f32, bid = self.plan[j]
            s = j % NS
            dst = wring[:, s, 0:kc * ncol].rearrange("p (k n) -> p k n", k=kc)
            bkey = "SB_%s_%s" % (bid[0], bid[1])
            nbytes = kc * ncol * 256
            if bid not in self.seen:
                self.seen.add(bid)
                P.op(POOL, lambda e: e.dma_start(out=dst, in_=f32), writes=["w%d" % s], dma=True, cost=2500.0 + 2 * nbytes / 150.0)
                P.op(SP, lambda e: e.dma_start(out=src, in_=dst), reads=["w%d" % s], writes=[bkey], dma=True, cost=2500.0 + nbytes / 150.0)
            else:
                P.op(SP, lambda e: e.dma_start(out=dst, in_=src), reads=[bkey], writes=["w%d" % s], dma=True,
                     cost=2000.0 + nbytes / 180.0)

        def next(self, src, kc, ncol, skeys, live=1, f32=None, bid=None):
            assert kc * ncol <= SLOT
            if self.planning:
                self.plan.append((src, kc, ncol, skeys, f32, bid))
                return None, None
            j = self.i
            self.i += 1
            while self.issued < min(len(self.plan), j + NS - live + 1):
                self._issue(self.issued)
                self.issued += 1
            assert self.issued > j
            s = j % NS
            return wring[:, s, 0:kc * ncol].rearrange("p (k n) -> p k n", k=kc), "w%d" % s

    ring = Ring()

    def REQ(*a, **k):
        return (a, k)

    def drive(gens):
        cur = []
        for g in gens:
            try:
                cur.append(next(g))
            except StopIteration:
                cur.append(None)
        while any(c_ is not None for c_ in cur):
            req = next(c_ for c_ in cur if c_ is not None)["a"][0]
            res = ring.next(*req[0], **req[1])
            for i_, g in enumerate(gens):
                if cur[i_] is None:
                    continue
                assert cur[i_]["a"][0][1]["bid"] == req[1]["bid"], (cur[i_]["a"][0][1]["bid"], req[1]["bid"])
                try:
                    cur[i_] = g.send(res)
                except StopIteration:
                    cur[i_] = None

    def wview(scr2d, c0, ncol):
        return scr2d.rearrange("(k p) n -> p k n", p=128)[:, :, c0:c0 + ncol]

    def mm(out, lhsT, rhs, start, stop, reads, writes):
        if ring.planning:
            return
        n = rhs.shape[-1]
        P.op(PE, lambda e: e.matmul(out, lhsT, rhs, start=start, stop=stop), reads=reads, writes=writes,
             cost=max(60.0, 16.0 + n / 2.2))

    def op(eng, fn, reads, writes, n=None):
        if ring.planning:
            return
        cost = None
        if eng == PE:
            cost = 70.0
        elif n is not None:
            cost = 100.0 + 1.15 * n
        t_rec = cur_tile[0]

        def fn2(e, fn=fn, t_rec=t_rec):
            old = cur_tile[0]
            cur_tile[0] = t_rec
            try:
                return fn(e)
            finally:
                cur_tile[0] = old
        P.op(eng, fn2, reads=reads, writes=writes, cost=cost)

    def dma(eng, dst, src, reads, writes, cost=None):
        if ring.planning:
            return None
        return P.op(eng, lambda e: e.dma_start(out=dst, in_=src), reads=reads, writes=writes, dma=True, cost=cost)

    out_dmas = []

    def fence(old, new):
        if ring.planning:
            return
        acc = []
        for k in old:
            w = P.last_w.get(k)
            if w is not None:
                acc.append(w)
            acc.extend(P.readers.get(k, []))
        uniq = []
        seen = set()
        for a in acc:
            if id(a) not in seen:
                seen.add(id(a))
                uniq.append(a)
        for k in new:
            P.readers.setdefault(k, []).extend(uniq)

    CH4 = [0, 1, 2, 3]
    K_U = ["uT%d_%d" % (q, c) for q in range(4) for c in CH4]
    K_A = ["aT%d_%d" % (j, c) for j in range(NFC) for c in CH4]
    K_F = ["bigF"]
    K_VN = ["vn%d" % c for c in CH4]
    K_Q = ["qT%d" % c for c in CH4] + ["qb0", "qb1", "Pm0", "Pm1"]

    def psT(b, ncols):
        return ps[:, b, :].bitcast(BF16)[:, 0:ncols]

    def rstd_from(c, rn, width):
        op(DVE, lambda e: e.tensor_scalar(out=stat[0:rn, c + 1:c + 2], in0=stat[0:rn, c:c + 1], scalar1=1.0 / width, scalar2=EPS,
                                          op0=ALU.mult, op1=ALU.add), ["st%d" % c], ["st%d" % (c + 1)], n=1)
        op(POOL, lambda e: e.tensor_tensor(out=stat[0:rn, c + 2:c + 3], in0=stat[0:rn, c + 1:c + 2], in1=nhalf[0:rn, :], op=ALU.pow),
           ["st%d" % (c + 1), "c_nhalf"], ["st%d" % (c + 2)], n=100)
        return c + 2

    def rms_stats(src_ap, rn, width, reads):
        c = scol(3)
        op(ACT, lambda e: e.activation(out=junk[0:rn, 0:width], in_=src_ap, func=AF.Square, accum_out=stat[0:rn, c:c + 1]),
           reads, GEK + ["st%d" % c])
        return rstd_from(c, rn, width)

    def norm_to_hT(ctx, nidx):
        rn = ctx["rn"]
        for c in ctx["chunks"]:
            r = rms_stats(x[0:rn, XS(c), :], rn, D, [XK(c)])
            hb = c % 2
            op(ACT, lambda e, c=c, r=r, hb=hb: e.activation(out=htok[0:rn, hb, :], in_=x[0:rn, XS(c), :], func=AF.Copy,
                                                           scale=stat[0:rn, r:r + 1]), [XK(c), "st%d" % r], ["htok%d" % hb], n=700)
            b = bank()
            for k in range(KC):
                op(PE, lambda e, k=k, b=b, hb=hb: e.transpose(psT(b, 1024)[:, k * rn:(k + 1) * rn], htok[0:rn, hb, k * 128:(k + 1) * 128],
                                                              identb[0:rn, 0:rn]), ["htok%d" % hb, "c_identb"], ["ps%d" % b])
            op(DVE, lambda e, b=b, c=c: e.tensor_tensor(
                out=hT[:, :, c * rn:(c + 1) * rn],
                in0=psT(b, KC * rn).rearrange("p (k t) -> p k t", k=KC),
                in1=nwfm[:, nidx, :].unsqueeze(2).to_broadcast([128, KC, rn]), op=ALU.mult),
               ["ps%d" % b, "c_nwfm"], ["hT%d" % c])

    def hT_keys(ctx):
        return ["hT%d" % c for c in ctx["chunks"]]

    def gmlp(ctx):
        rn, chunks, ntok = ctx["rn"], ctx["chunks"], ctx["ntok"]
        tk0 = ctx.get("t0", 0)
        sample = ctx["sample"]
        norm_to_hT(ctx, 0)
        sc = {c: scol(4) for c in chunks}
        def vfs(i):
            return (x[0:64, 4, i * 512:(i + 1) * 512], ["x4"]) if i < 2 else (thflat[0:64, (i - 2) * 512:(i - 1) * 512], TH)

        def vfs2(hf):
            return (x[0:64, 4, :], ["x4"]) if hf == 0 else (thflat[0:64, :], TH)
        for i in range(4):
            wv, wk = yield dict(a=(REQ(wview(s_w_in, GH + i * 512, 512), KC, 512, SK["in"], f32=wview(w_in_d, GH + i * 512, 512), bid=("in", GH + i * 512)),))
            for c in chunks:
                b = bank()
                for k in range(KC):
                    mm(ps[0:rn, b, :], hT[:, k, c * rn:(c + 1) * rn], None if wv is None else wv[:, k, :], k == 0, k == KC - 1,
                       ["hT%d" % c, wk], ["ps%d" % b])
                col = i * 512
                cc = sc[c] + i
                if sample:
                    vap, vk = vfs(i)
                    op(ACT, lambda e, b=b, vap=vap: e.activation(out=vap, in_=ps[0:rn, b, :], func=AF.Gelu_apprx_tanh),
                       ["ps%d" % b], vk)
                    op(ACT, lambda e, vap=vap, cc=cc: e.activation(out=junk[0:rn, 0:512], in_=vap, func=AF.Square,
                                                                   accum_out=stat[0:rn, cc:cc + 1]), vk, GEK + ["st%d" % cc])
                else:
                    op(ACT, lambda e, b=b, col=col, c=c: e.activation(out=vn[0:rn, c, col:col + 512], in_=ps[0:rn, b, :], func=AF.Gelu_apprx_tanh),
                       ["ps%d" % b], ["vn%d" % c])
                    op(ACT, lambda e, col=col, cc=cc, c=c: e.activation(out=junk[0:rn, 0:512], in_=vn[0:rn, c, col:col + 512], func=AF.Square,
                                                                        accum_out=stat[0:rn, cc:cc + 1]), ["vn%d" % c], GEK + ["st%d" % cc])
        for c in chunks:
            s0 = sc[c]
            c3 = scol(3)
            op(DVE, lambda e, s0=s0, c3=c3: e.tensor_reduce(out=stat[0:rn, c3:c3 + 1], in_=stat[0:rn, s0:s0 + 4], axis=AX.X, op=ALU.add),
               ["st%d" % (s0 + i) for i in range(4)], ["st%d" % c3])
            r = rstd_from(c3, rn, GH)
            if sample:
                for hf in range(2):
                    vap, vk = vfs2(hf)
                    op(DVE, lambda e, r=r, vap=vap, hf=hf: e.scalar_tensor_tensor(out=vap, in0=vap, scalar=stat[0:rn, r:r + 1],
                                                                                in1=vnw[0:rn, hf * 1024:(hf + 1) * 1024], op0=ALU.mult, op1=ALU.mult),
                       vk + ["st%d" % r, "c_vnw"], vk)
                    out_dmas.append(dma(POOL, gmv_d[:, hf * 1024:(hf + 1) * 1024], vap, vk, []))
                    op(DVE, lambda e, c=c, vap=vap, hf=hf: e.tensor_copy(out=vn[0:rn, c, hf * 1024:(hf + 1) * 1024], in_=vap), vk, ["vn%d" % c])
            else:
                op(DVE, lambda e, r=r, c=c: e.scalar_tensor_tensor(out=vn[0:rn, c, :], in0=vn[0:rn, c, :], scalar=stat[0:rn, r:r + 1],
                                                                  in1=vnw[0:rn, :], op0=ALU.mult, op1=ALU.mult),
                   ["vn%d" % c, "st%d" % r, "c_vnw"], ["vn%d" % c])
        for i in range(4):
            wv, wk = yield dict(a=(REQ(wview(s_w_in, i * 512, 512), KC, 512, SK["in"], f32=wview(w_in_d, i * 512, 512), bid=("in", i * 512)),))
            for j in range(4):
                jj = i * 4 + j
                b = bank()
                for k in range(KC):
                    mm(ps[:, b, 0:ntok], None if wv is None else wv[:, k, j * 128:(j + 1) * 128], hT[:, k, tk0:tk0 + ntok], k == 0, k == KC - 1,
                       hT_keys(ctx) + [wk], ["ps%d" % b])
                op(ACT, lambda e, b=b, jj=jj: e.activation(out=big[:, jj, tk0:tk0 + ntok], in_=ps[:, b, 0:ntok], func=AF.Gelu_apprx_tanh),
                   ["ps%d" % b], ["uT%d_%d" % (jj // 4, c) for c in chunks])
        wsm, wsk = (wsTs, "c_wsTs") if sample else (wsT, "c_wsT")
        bsm, bsk = (bss, "c_bss") if sample else (bsp, "c_bsp")
        for c in chunks:
            for q4 in range(4):
                b = bank()
                for qq in range(4):
                    cc = q4 * 4 + qq
                    g = cc // 2
                    mm(ps[:, b, qq * rn:(qq + 1) * rn], vn[0:rn, c, cc * 128:(cc + 1) * 128], wsm[0:rn, g, 0:rn], True, False,
                       ["vn%d" % c, wsk], ["ps%d" % b])
                    mm(ps[:, b, qq * rn:(qq + 1) * rn], ones[0:1, :], bsm[0:1, g, 0:rn], False, True, ["c_ones", bsk], ["ps%d" % b])
                op(DVE, lambda e, b=b, q4=q4, c=c: e.tensor_tensor(
                    out=big[:, q4 * 4:(q4 + 1) * 4, c * rn:(c + 1) * rn],
                    in0=ps[:, b, 0:4 * rn].rearrange("p (q t) -> p q t", q=4),
                    in1=big[:, q4 * 4:(q4 + 1) * 4, c * rn:(c + 1) * rn], op=ALU.mult),
                   ["ps%d" % b, "uT%d_%d" % (q4, c)], ["uT%d_%d" % (q4, c)])
        for n in range(2):
            banks = {}
            for kh in range(2):
                wv, wk = yield dict(a=(REQ(wview(s_w_out, n * 512, 512)[:, kh * 8:(kh + 1) * 8, :], 8, 512, SK["out"], f32=wview(w_out_d, n * 512, 512)[:, kh * 8:(kh + 1) * 8, :], bid=("out", n * 512 + kh)),))
                for c in chunks:
                    if kh == 0:
                        banks[c] = bank()
                    b = banks[c]
                    for k in range(8):
                        kk = kh * 8 + k
                        mm(ps[0:rn, b, :], big[:, kk, c * rn:(c + 1) * rn], None if wv is None else wv[:, k, :], kk == 0, kk == 15,
                           ["uT%d_%d" % (kk // 4, c), wk], ["ps%d" % b])
                    if kh == 1:
                        op(DVE, lambda e, b=b, c=c, n=n: e.tensor_tensor(out=x[0:rn, XS(c), n * 512:(n + 1) * 512], in0=ps[0:rn, b, :],
                                                                        in1=x[0:rn, XS(c), n * 512:(n + 1) * 512], op=ALU.add),
                           ["ps%d" % b, XK(c)], [XK(c)])

    def ffn(ctx, l):
        rn, chunks, ntok = ctx["rn"], ctx["chunks"], ctx["ntok"]
        tk0 = ctx.get("t0", 0)
        nseq, L = ctx["nseq"], ctx["L"]
        sample = ctx["sample"]
        norm_to_hT(ctx, 1 if l == 0 else 5)
        gk = "gst%d" % l
        ns2 = nseq * 2
        gstx = gsts if sample else gstp
        for i in range(6):
            ncol = 512 if i < 5 else 256
            gv, gkk = yield dict(a=(REQ(wview(s_w_gate[l], i * 512, ncol), KC, ncol, SK["gate%d" % l], live=1, f32=wview(w_gate_d[l], i * 512, ncol), bid=("gate%d" % l, i * 512)),))
            uv, ukk = yield dict(a=(REQ(wview(s_w_up[l], i * 512, ncol), KC, ncol, SK["up%d" % l], live=2, f32=wview(w_up_d[l], i * 512, ncol), bid=("up%d" % l, i * 512)),))
            for j in range(ncol // 128):
                jj = i * 4 + j
                bg = bank()
                for k in range(KC):
                    mm(ps[:, bg, 0:ntok], None if gv is None else gv[:, k, j * 128:(j + 1) * 128], hT[:, k, tk0:tk0 + ntok], k == 0, k == KC - 1,
                       hT_keys(ctx) + [gkk], ["ps%d" % bg])
                bu = bank()
                for k in range(KC):
                    mm(ps[:, bu, 0:ntok], None if uv is None else uv[:, k, j * 128:(j + 1) * 128], hT[:, k, tk0:tk0 + ntok], k == 0, k == KC - 1,
                       hT_keys(ctx) + [ukk], ["ps%d" % bu])
                cbi = jj % 2
                G3 = ps[:, bg, 0:ntok].rearrange("p (s l) -> p s l", s=nseq)
                C3 = cb[:, cbi, 0:ntok].rearrange("p (s l) -> p s l", s=nseq)
                GP3 = gp[:, cbi, 0:nseq * (L + 2)].rearrange("p (s l) -> p s l", s=nseq)
                S3 = gstx[:, l, jj, 0:ns2].rearrange("p (s k) -> p s k", s=nseq)
                w0 = convw[:, l, jj, 0:1]
                w1 = convw[:, l, jj, 1:2]
                w2 = convw[:, l, jj, 2:3]
                bb = convw[:, l, jj, 3:4]
                ck = "cb%d" % cbi
                gpk = "gp%d" % cbi
                op(POOL, lambda e, GP3=GP3, S3=S3: e.tensor_copy(out=GP3[:, :, 0:2], in_=S3), [gk], [gpk])
                op(ACT, lambda e, GP3=GP3, G3=G3: e.activation(out=GP3[:, :, 2:L + 2], in_=G3, func=AF.Copy), ["ps%d" % bg], [gpk])
                op(ACT, lambda e, cbi=cbi, bu=bu: e.activation(out=ub[:, cbi, 0:ntok], in_=ps[:, bu, 0:ntok], func=AF.Copy), ["ps%d" % bu], ["ub%d" % cbi])
                op(DVE, lambda e, GP3=GP3, C3=C3, w2=w2, bb=bb: e.tensor_scalar(out=C3, in0=GP3[:, :, 2:L + 2], scalar1=w2, scalar2=bb,
                                                                             op0=ALU.mult, op1=ALU.add), [gpk, "c_convw"], [ck])
                op(DVE, lambda e, GP3=GP3, C3=C3, w1=w1: e.scalar_tensor_tensor(out=C3, in0=GP3[:, :, 1:L + 1], scalar=w1, in1=C3,
                                                                               op0=ALU.mult, op1=ALU.add), [gpk, ck, "c_convw"], [ck])
                op(DVE, lambda e, GP3=GP3, C3=C3, w0=w0: e.scalar_tensor_tensor(out=C3, in0=GP3[:, :, 0:L], scalar=w0, in1=C3,
                                                                               op0=ALU.mult, op1=ALU.add), [gpk, ck, "c_convw"], [ck])
                if sample:
                    N3 = gnew[:, jj, 0:ns2].rearrange("p (s k) -> p s k", s=nseq)
                    op(POOL, lambda e, GP3=GP3, N3=N3: e.tensor_copy(out=N3, in_=GP3[:, :, L:L + 2]), [gpk], ["gnew"])
                else:
                    op(POOL, lambda e, GP3=GP3, S3=S3: e.tensor_copy(out=S3, in_=GP3[:, :, L:L + 2]), [gpk, gk], [gk])
                op(ACT, lambda e, cbi=cbi: e.activation(out=ge[:, cbi, 0:ntok], in_=cb[:, cbi, 0:ntok], func=AF.Gelu_apprx_tanh),
                   [ck], ["ge%d" % cbi])
                op(DVE, lambda e, cbi=cbi, jj=jj: e.tensor_tensor(out=big[:, jj, tk0:tk0 + ntok], in0=ub[:, cbi, 0:ntok], in1=ge[:, cbi, 0:ntok], op=ALU.mult),
                   ["ub%d" % cbi, "ge%d" % cbi], ["aT%d_%d" % (jj, c) for c in chunks])
        for n in range(2):
            banks = {}
            for kh in range(2):
                wv, wk = yield dict(a=(REQ(wview(s_w_down[l], n * 512, 512)[:, kh * 11:(kh + 1) * 11, :], 11, 512, SK["down%d" % l], f32=wview(w_down_d[l], n * 512, 512)[:, kh * 11:(kh + 1) * 11, :], bid=("down%d" % l, n * 512 + kh)),))
                for c in chunks:
                    if kh == 0:
                        banks[c] = bank()
                    b = banks[c]
                    for k in range(11):
                        kk = kh * 11 + k
                        mm(ps[0:rn, b, :], big[:, kk, c * rn:(c + 1) * rn], None if wv is None else wv[:, k, :], kk == 0, kk == NFC - 1,
                           ["aT%d_%d" % (kk, c), wk], ["ps%d" % b])
                    if kh == 1:
                        op(DVE, lambda e, b=b, c=c, n=n: e.tensor_tensor(out=x[0:rn, XS(c), n * 512:(n + 1) * 512], in0=ps[0:rn, b, :],
                                                                        in1=x[0:rn, XS(c), n * 512:(n + 1) * 512], op=ALU.add),
                           ["ps%d" % b, XK(c)], [XK(c)])

    def ple(ctx, l):
        rn, chunks = ctx["rn"], ctx["chunks"]
        norm_to_hT(ctx, 2 if l == 0 else 6)
        for c in chunks:
            pbuf = c % 2
            dma(SP, pf[0:rn, pbuf, :], ctx["p_src"](l, c), [], ["pf%d" % pbuf])
            op(DVE, lambda e, pbuf=pbuf: e.tensor_copy(out=pb[0:rn, pbuf, :], in_=pf[0:rn, pbuf, :]), ["pf%d" % pbuf], ["pb%d" % pbuf])
            b = bank()
            for k in range(2):
                op(PE, lambda e, k=k, b=b, pbuf=pbuf: e.transpose(psT(b, 1024)[:, k * rn:(k + 1) * rn], pb[0:rn, pbuf, k * 128:(k + 1) * 128],
                                                                  identb[0:rn, 0:rn]), ["pb%d" % pbuf, "c_identb"], ["ps%d" % b])
            op(DVE, lambda e, b=b, c=c: e.tensor_copy(out=pT[:, :, c * rn:(c + 1) * rn], in_=psT(b, 2 * rn).rearrange("p (k t) -> p k t", k=2)),
               ["ps%d" % b], ["pT%d" % c])
        pv, pk_ = yield dict(a=(REQ(wview(s_w_pp[l], 0, 1024), 2, 1024, SK["pp%d" % l], live=1, f32=wview(w_pp_d[l], 0, 1024), bid=("pp%d" % l, 0)),))
        for n in range(2):
            gv, gk_ = yield dict(a=(REQ(wview(s_w_pg[l], n * 512, 512), KC, 512, SK["pg%d" % l], live=n + 2, f32=wview(w_pg_d[l], n * 512, 512), bid=("pg%d" % l, n * 512)),))
            for c in chunks:
                tb = c % 2
                b1 = bank()
                for k in range(KC):
                    mm(ps[0:rn, b1, :], hT[:, k, c * rn:(c + 1) * rn], None if gv is None else gv[:, k, :], k == 0, k == KC - 1,
                       ["hT%d" % c, gk_], ["ps%d" % b1])
                op(ACT, lambda e, b1=b1, tb=tb: e.activation(out=th[0:rn, tb, :], in_=ps[0:rn, b1, :], func=AF.Tanh, scale=0.5),
                   ["ps%d" % b1], ["th%d" % tb])
                b2 = bank()
                for k in range(2):
                    mm(ps[0:rn, b2, :], pT[:, k, c * rn:(c + 1) * rn], None if pv is None else pv[:, k, n * 512:(n + 1) * 512], k == 0, k == 1,
                       ["pT%d" % c, pk_], ["ps%d" % b2])
                op(DVE, lambda e, b2=b2, tb=tb: e.scalar_tensor_tensor(out=th[0:rn, tb, :], in0=th[0:rn, tb, :], scalar=1.0, in1=ps[0:rn, b2, :],
                                                                      op0=ALU.add, op1=ALU.mult), ["ps%d" % b2, "th%d" % tb], ["th%d" % tb])
                op(DVE, lambda e, tb=tb, c=c, n=n: e.scalar_tensor_tensor(out=x[0:rn, XS(c), n * 512:(n + 1) * 512], in0=th[0:rn, tb, :], scalar=0.5,
                                                                         in1=x[0:rn, XS(c), n * 512:(n + 1) * 512], op0=ALU.mult, op1=ALU.add),
                   ["th%d" % tb, XK(c)], [XK(c)])

    def rope(src3, dst3, rn, nh, c, rkeys, wkeys):
        cos = rope_t[0:rn, c, 0:8].unsqueeze(1).to_broadcast([rn, nh, 8])
        sin = rope_t[0:rn, c, 8:16].unsqueeze(1).to_broadcast([rn, nh, 8])
        t = [rt[0:rn, i, 0:nh, :] for i in range(4)]
        tk = ["rt%d" % i for i in range(4)]
        x1 = src3[:, :, 0:8]
        x2 = src3[:, :, 8:16]
        op(DVE, lambda e: e.tensor_tensor(out=t[0], in0=x1, in1=cos, op=ALU.mult), rkeys + ["rope_t"], [tk[0]], n=64)
        op(DVE, lambda e: e.tensor_tensor(out=t[1], in0=x2, in1=sin, op=ALU.mult), rkeys + ["rope_t"], [tk[1]], n=64)
        op(DVE, lambda e: e.tensor_tensor(out=t[2], in0=x2, in1=cos, op=ALU.mult), rkeys + ["rope_t"], [tk[2]], n=64)
        op(DVE, lambda e: e.tensor_tensor(out=t[3], in0=x1, in1=sin, op=ALU.mult), rkeys + ["rope_t"], [tk[3]], n=64)
        op(DVE, lambda e: e.tensor_tensor(out=dst3[:, :, 0:8], in0=t[0], in1=t[1], op=ALU.subtract), [tk[0], tk[1]], wkeys, n=64)
        op(DVE, lambda e: e.tensor_tensor(out=dst3[:, :, 8:16], in0=t[2], in1=t[3], op=ALU.add), [tk[2], tk[3]], wkeys, n=64)

    def shared_kv(ctx):
        rn, chunks = ctx["rn"], ctx["chunks"]
        norm_to_hT(ctx, 3)
        wv, wk = yield dict(a=(REQ(wview(s_w_kv, 0, 256), KC, 256, SK["kv"], f32=wview(w_kv_d, 0, 256), bid=("kv", 0)),))
        for c in chunks:
            kb_ = c % 2
            b = bank()
            for k in range(KC):
                mm(ps[0:rn, b, 0:256], hT[:, k, c * rn:(c + 1) * rn], None if wv is None else wv[:, k, :], k == 0, k == KC - 1,
                   ["hT%d" % c, wk], ["ps%d" % b])
            op(ACT, lambda e, b=b, kb_=kb_: e.activation(out=kvf[0:rn, kb_, :], in_=ps[0:rn, b, 0:256], func=AF.Copy), ["ps%d" % b], ["kvf%d" % kb_])
            k3p = ps[0:rn, b, 0:128].rearrange("p (h d) -> p h d", h=2)
            k3 = kvf[0:rn, kb_, 0:128].rearrange("p (h d) -> p h d", h=2)
            rope(k3, k3, rn, 2, c, ["kvf%d" % kb_], ["kvf%d" % kb_])
            if ctx["sample"]:
                out_dmas.append(dma(POOL, nks_d, kvf[0:rn, kb_, 0:128], ["kvf%d" % kb_], []))
                out_dmas.append(dma(POOL, nvs_d, kvf[0:rn, kb_, 128:256], ["kvf%d" % kb_], []))
                op(DVE, lambda e, kb_=kb_: e.tensor_copy(out=kvb[:, :], in_=kvf[0:rn, kb_, :]), ["kvf%d" % kb_], ["kvb"])
                dma(POOL, s_kv, kvb[:, :], ["kvb"], ["s_kv"])
                dma(POOL, knew4[:, :, :], s_kv.rearrange("(b t) f -> t b f", t=4)[:, :, 128:256], ["s_kv"], ["knew4"])
                bt = bank()
                op(PE, lambda e, bt=bt: e.transpose(psT(bt, 64), kvb[:, 0:128], identb[0:64, 0:64]), ["kvb", "c_identb"], ["ps%d" % bt])
                op(DVE, lambda e, bt=bt: e.tensor_copy(out=kTn[:, :], in_=psT(bt, 64)), ["ps%d" % bt], ["kTnew"])
            else:
                if ctx["last"] and c == chunks[-1]:
                    out_dmas.append(dma(POOL, nkp_d, kvf[0:rn, kb_, 0:128], ["kvf%d" % kb_], []))
                    out_dmas.append(dma(POOL, nvp_d, kvf[0:rn, kb_, 128:256], ["kvf%d" % kb_], []))
                op(DVE, lambda e, kb_=kb_: e.tensor_copy(
                    out=kb2[0:rn, kb_, :].rearrange("p (g u d) -> p g u d", g=2, u=2),
                    in_=kvf[0:rn, kb_, 0:128].rearrange("p (g d) -> p g d", g=2).unsqueeze(2).to_broadcast([rn, 2, 2, 64])),
                   ["kvf%d" % kb_], ["kb2_%d" % kb_], n=256)
                op(ACT, lambda e, kb_=kb_, c=c: e.activation(out=vtok[0:rn, c + 1, :], in_=kvf[0:rn, kb_, 128:256], func=AF.Copy),
                   ["kvf%d" % kb_], ["vt%d" % (c + 1)])
                bt = bank()
                for g in range(2):
                    op(PE, lambda e, g=g, bt=bt, kb_=kb_: e.transpose(psT(bt, 256)[:, g * 128:(g + 1) * 128], kb2[0:rn, kb_, g * 128:(g + 1) * 128],
                                                                      identb[0:rn, 0:rn]), ["kb2_%d" % kb_, "c_identb"], ["ps%d" % bt])
                op(DVE, lambda e, bt=bt, c=c: e.tensor_copy(out=kTd[:, :, (c + 1) * 128:(c + 2) * 128],
                                                            in_=psT(bt, 256).rearrange("p (g t) -> p g t", g=2)), ["ps%d" % bt], ["kT%d" % (c + 1)])

    def q_proj(ctx):
        rn, chunks = ctx["rn"], ctx["chunks"]
        norm_to_hT(ctx, 4)
        qw = []
        for n in range(2):
            r_ = yield dict(a=(REQ(wview(s_w_q, n * 512, 512), KC, 512, SK["q"], live=n + 1, f32=wview(w_q_d, n * 512, 512), bid=("q", n * 512)),))
            qw.append(r_)
        for c in chunks:
            qp = c % 2
            for n in range(2):
                wv, wk = qw[n]
                b = bank()
                for k in range(KC):
                    mm(ps[0:rn, b, :], hT[:, k, c * rn:(c + 1) * rn], None if wv is None else wv[:, k, :], k == 0, k == KC - 1,
                       ["hT%d" % c, wk], ["ps%d" % b])
                op(ACT, lambda e, b=b, qp=qp, n=n: e.activation(out=qb[0:rn, qp, n * 512:(n + 1) * 512], in_=ps[0:rn, b, :], func=AF.Copy),
                   ["ps%d" % b], ["qb%d" % qp])
                q3p = ps[0:rn, b, :].rearrange("p (h d) -> p h d", h=8)
                q3 = qb[0:rn, qp, n * 512:(n + 1) * 512].rearrange("p (h d) -> p h d", h=8)
                rope(q3, q3, rn, 8, c, ["qb%d" % qp], ["qb%d" % qp])
            bt = bank()
            if ctx["sample"]:
                for h in range(16):
                    g, hl = h // 8, h % 8
                    op(PE, lambda e, h=h, g=g, hl=hl, bt=bt, qp=qp: e.transpose(psT(bt, 512)[g * 64:(g + 1) * 64, hl * 64:(hl + 1) * 64],
                                                                              qb[0:rn, qp, h * 64:(h + 1) * 64], identb[0:64, 0:64]),
                       ["qb%d" % qp, "c_identb"], ["ps%d" % bt])
                op(DVE, lambda e: e.memset(qT2[:], 0.0), [], ["qT2"])
                op(DVE, lambda e, bt=bt: e.tensor_copy(out=qtmp[:, :], in_=psT(bt, 512)), ["ps%d" % bt], ["qtmp"])
                for g in range(2):
                    op(DVE, lambda e, g=g: e.tensor_copy(
                        out=qT2[g * 64:(g + 1) * 64, :, g * 32:(g + 1) * 32].rearrange("p b (h t) -> p h b t", h=8),
                        in_=qtmp[g * 64:(g + 1) * 64, :].rearrange("p (h b t) -> p h b t", h=8, b=16)), ["qtmp"], ["qT2"])
            else:
                for k in range(KC):
                    op(PE, lambda e, k=k, bt=bt, qp=qp: e.transpose(psT(bt, 1024)[:, k * 128:(k + 1) * 128], qb[0:rn, qp, k * 128:(k + 1) * 128],
                                                                    identb[:, :]), ["qb%d" % qp, "c_identb"], ["ps%d" % bt])
                op(DVE, lambda e, bt=bt, c=c: e.tensor_copy(out=qT[:, :, c * 128:(c + 1) * 128], in_=psT(bt, 1024).rearrange("p (k t) -> p k t", k=KC)),
                   ["ps%d" % bt], ["qT%d" % c])

    def attn_s1(ctx, c, hg, par):
        mk, mkk = (maskb, "c_maskb") if (ctx["first"] and c == 0) else (maska, "c_maska")
        bs = 2 + 2 * par
        S = ps[:, bs:bs + 2, :].rearrange("p a (h k) -> p (a h) k", h=2)
        skeys = ["ps%d" % bs, "ps%d" % (bs + 1)]
        for hi in range(4):
            h = hg * 4 + hi
            g, kc_, hh = h // 8, h // 2, h % 2
            mm(S[:, hi, :], qT[hh * 64:(hh + 1) * 64, kc_, c * 128:(c + 1) * 128], kTd[hh * 64:(hh + 1) * 64, g, c * 128:c * 128 + 256],
               True, False, ["qT%d" % c, "kT%d" % c, "kT%d" % (c + 1)], [skeys[hi // 2]])
            mm(S[:, hi, :], identb[:, :], mk[:, :], False, True, ["c_identb", mkk], [skeys[hi // 2]])

    def attn_s1b(ctx, c, hg, par):
        bs = 2 + 2 * par
        S = ps[:, bs:bs + 2, :].rearrange("p a (h k) -> p (a h) k", h=2)
        skeys = ["ps%d" % bs, "ps%d" % (bs + 1)]
        A = att[:, par]
        ak = "att%d" % par
        op(DVE, lambda e, S=S, A=A: e.tensor_reduce(out=A[:, 0, :], in_=S, axis=AX.X, op=ALU.max), skeys, [ak])
        op(DVE, lambda e, A=A, hg=hg: e.tensor_tensor(out=A[:, 1, :], in0=A[:, 0, :], in1=sink8[:, hg * 4:(hg + 1) * 4], op=ALU.max),
           [ak, "c_sink8"], [ak], n=4)
        op(DVE, lambda e, A=A: e.tensor_scalar(out=A[:, 2, :], in0=A[:, 1, :], scalar1=-0.125, scalar2=None, op0=ALU.mult), [ak], [ak], n=4)
        for hi in range(4):
            op(ACT, lambda e, S=S, A=A, hi=hi, par=par: e.activation(out=Pm[:, par, hi, :], in_=S[:, hi, :], func=AF.Exp, bias=A[:, 2, hi:hi + 1],
                                                                    scale=0.125, accum_out=A[:, 3, hi:hi + 1]),
               skeys + [ak], ["Pm%d" % par, ak])
        op(DVE, lambda e, A=A, hg=hg: e.scalar_tensor_tensor(out=A[:, 4, :], in0=sink8[:, hg * 4:(hg + 1) * 4], scalar=0.125, in1=A[:, 2, :],
                                                            op0=ALU.mult, op1=ALU.add), [ak, "c_sink8"], [ak])
        op(ACT, lambda e, A=A: e.activation(out=A[:, 4, :], in_=A[:, 4, :], func=AF.Exp), [ak], [ak])
        op(DVE, lambda e, A=A: e.tensor_tensor(out=A[:, 4, :], in0=A[:, 4, :], in1=A[:, 3, :], op=ALU.add), [ak], [ak], n=4)
        op(DVE, lambda e, A=A: e.reciprocal(out=A[:, 5, :], in_=A[:, 4, :]), [ak], [ak], n=4)
        op(DVE, lambda e, A=A, par=par: e.tensor_tensor(out=Dg[:, par], in0=identf[:, :].unsqueeze(1).to_broadcast([128, 4, 128]),
                                                       in1=A[:, 5, :].unsqueeze(2).to_broadcast([128, 4, 128]), op=ALU.mult),
           [ak, "c_identf"], ["Dg%d" % par])

    def attn_s2(ctx, c, hg, par):
        bo = 0
        bp = 6
        PTp = ps[:, bp:bp + 2, :].rearrange("p a (h k) -> p (a h) k", h=2)
        pkeys = ["ps%d" % bp, "ps%d" % (bp + 1)]
        for hi in range(4):
            for blk in range(2):
                mm(PTp[:, hi, blk * 128:(blk + 1) * 128], Pm[:, par, hi, blk * 128:(blk + 1) * 128], Dg[:, par, hi, :], True, True,
                   ["Pm%d" % par, "Dg%d" % par], [pkeys[hi // 2]])
        PT3 = PT[:, par].rearrange("p h b t -> p h (b t)")
        op(ACT, lambda e, PTp=PTp, PT3=PT3: e.activation(out=PT3[:, 0:2, :], in_=PTp[:, 0:2, :], func=AF.Copy), [pkeys[0]], ["PT%d_0" % par])
        op(DVE, lambda e, PTp=PTp, PT3=PT3: e.tensor_copy(out=PT3[:, 2:4, :], in_=PTp[:, 2:4, :]), [pkeys[1]], ["PT%d_1" % par])
        for hi in range(4):
            h = hg * 4 + hi
            g, kc_, hh = h // 8, h // 2, h % 2
            for blk in range(2):
                mm(ps[hh * 64:(hh + 1) * 64, bo + kc_ // 4, (kc_ % 4) * 128:(kc_ % 4 + 1) * 128], vtok[:, c + blk, g * 64:(g + 1) * 64],
                   PT[:, par, hi, blk, :], blk == 0, blk == 1, ["vt%d" % (c + blk), "PT%d_%d" % (par, hi // 2)], ["ps%d" % (bo + kc_ // 4)])

    def attn_layer_prompt(ctx):
        rn, chunks = ctx["rn"], ctx["chunks"]
        yield from q_proj(ctx)
        ow = []
        for n in range(2):
            r_ = yield dict(a=(REQ(wview(s_w_o, n * 512, 512), KC, 512, SK["o"], live=n + 1, f32=wview(w_o_d, n * 512, 512), bid=("o", n * 512)),))
            ow.append(r_)
        groups = [(c, hg) for c in chunks for hg in range(4)]

        def finish_chunk(c):
            op_par = c % 2
            op(DVE, lambda e, op_par=op_par: e.tensor_copy(out=oT[:, op_par], in_=ps[:, 0:2, :].rearrange("p a (k t) -> p (a k) t", k=4)),
               ["ps0", "ps1"], ["oT%d" % op_par])
            for n in range(2):
                wv, wk = ow[n]
                b = 6 + n
                for k in range(KC):
                    mm(ps[0:rn, b, :], oT[:, op_par, k, :], None if wv is None else wv[:, k, :], k == 0, k == KC - 1,
                       ["oT%d" % op_par, wk], ["ps%d" % b])
                op(DVE, lambda e, b=b, c=c, n=n: e.tensor_tensor(out=x[0:rn, XS(c), n * 512:(n + 1) * 512], in0=ps[0:rn, b, :],
                                                                in1=x[0:rn, XS(c), n * 512:(n + 1) * 512], op=ALU.add),
                   ["ps%d" % b, XK(c)], [XK(c)])

        attn_s1(ctx, groups[0][0], groups[0][1], 0)
        attn_s1b(ctx, groups[0][0], groups[0][1], 0)
        for i, (c, hg) in enumerate(groups):
            if i + 1 < len(groups):
                attn_s1(ctx, groups[i + 1][0], groups[i + 1][1], (i + 1) % 2)
            attn_s2(ctx, c, hg, i % 2)
            if i + 1 < len(groups):
                attn_s1b(ctx, groups[i + 1][0], groups[i + 1][1], (i + 1) % 2)
            if hg == 3:
                finish_chunk(c)

    def attn_layer_sample(ctx):
        rn = 64
        yield from q_proj(ctx)
        ow = []
        for n in range(2):
            r_ = yield dict(a=(REQ(wview(s_w_o, n * 512, 512), KC, 512, SK["o"], live=n + 1, f32=wview(w_o_d, n * 512, 512), bid=("o", n * 512)),))
            ow.append(r_)
        for bi in range(16):
            par = bi % 2
            bt = bank()
            op(PE, lambda e, bt=bt, bi=bi: e.transpose(psT(bt, 128), ckb[:, bi, :], identb[:, :]), ["ckb_all", "c_identb"], ["ps%d" % bt])
            op(DVE, lambda e, bt=bt, par=par: e.tensor_copy(out=kTc[:, par, :], in_=psT(bt, 128)), ["ps%d" % bt], ["kTc%d" % par])
            bs = bank()
            mm(ps[0:64, bs, 0:132], identb[0:64, 0:64], masks[:, :], True, False, ["c_identb", "c_masks"], ["ps%d" % bs])
            mm(ps[0:64, bs, 0:128], qT2[:, bi, :], kTc[:, par, :], False, False, ["qT2", "kTc%d" % par], ["ps%d" % bs])
            mm(ps[0:64, bs, 128:132], qT2[:, bi, :], kTn[:, bi * 4:(bi + 1) * 4], False, True, ["qT2", "kTnew"], ["ps%d" % bs])
            A = att[0:64, par]
            ak = "att%d" % par
            S = ps[0:64, bs, 0:132]
            op(DVE, lambda e, S=S, A=A: e.tensor_reduce(out=A[:, 0, 0:1], in_=S, axis=AX.X, op=ALU.max), ["ps%d" % bs], [ak])
            op(DVE, lambda e, A=A: e.tensor_tensor(out=A[:, 1, 0:1], in0=A[:, 0, 0:1], in1=sinkr[:, 1:2], op=ALU.max), [ak, "c_sinkr8"], [ak])
            op(DVE, lambda e, A=A: e.tensor_scalar(out=A[:, 2, 0:1], in0=A[:, 1, 0:1], scalar1=-0.125, scalar2=None, op0=ALU.mult), [ak], [ak], n=4)
            op(ACT, lambda e, S=S, A=A, par=par: e.activation(out=Pm[0:64, par, 0, 0:132], in_=S, func=AF.Exp, bias=A[:, 2, 0:1], scale=0.125,
                                                             accum_out=A[:, 3, 0:1]), ["ps%d" % bs, ak], ["Pm%d" % par, ak])
            op(DVE, lambda e, A=A: e.tensor_tensor(out=A[:, 4, 0:1], in0=sinkr[:, 0:1], in1=A[:, 2, 0:1], op=ALU.add), [ak, "c_sinkr"], [ak])
            op(ACT, lambda e, A=A: e.activation(out=A[:, 4, 0:1], in_=A[:, 4, 0:1], func=AF.Exp), [ak], [ak])
            op(DVE, lambda e, A=A: e.tensor_tensor(out=A[:, 4, 0:1], in0=A[:, 4, 0:1], in1=A[:, 3, 0:1], op=ALU.add), [ak], [ak], n=4)
            op(DVE, lambda e, A=A: e.reciprocal(out=A[:, 5, 0:1], in_=A[:, 4, 0:1]), [ak], [ak])
            op(DVE, lambda e, A=A, par=par: e.tensor_scalar(out=Dg[0:64, par, 0, 0:64], in0=identf[0:64, 0:64], scalar1=A[:, 5, 0:1], scalar2=None,
                                                           op0=ALU.mult), [ak, "c_identf"], ["Dg%d" % par])
            bp = bank()
            mm(ps[:, bp, 0:64], Pm[0:64, par, 0, 0:128], Dg[0:64, par, 0, 0:64], True, True, ["Pm%d" % par, "Dg%d" % par], ["ps%d" % bp])
            mm(ps[0:4, bp, 64:128], Pm[0:64, par, 0, 128:132], Dg[0:64, par, 0, 0:64], True, True, ["Pm%d" % par, "Dg%d" % par], ["ps%d" % bp])
            op(ACT, lambda e, bp=bp, par=par: e.activation(out=PTs[:, par, :], in_=ps[:, bp, 0:64], func=AF.Copy), ["ps%d" % bp], ["PTs%d" % par])
            op(ACT, lambda e, bp=bp, par=par: e.activation(out=PTn[:, par, :], in_=ps[0:4, bp, 64:128], func=AF.Copy), ["ps%d" % bp], ["PTn%d" % par])
            bo = bank()
            mm(ps[0:64, bo, 0:128], PTs[:, par, :], cvb[:, bi, :], True, False, ["PTs%d" % par, "cvb_all"], ["ps%d" % bo])
            mm(ps[0:64, bo, 0:128], PTn[:, par, :], knew4[0:4, bi, :], False, True, ["PTn%d" % par, "knew4"], ["ps%d" % bo])
            op(DVE, lambda e, bo=bo, bi=bi: e.tensor_copy(out=o_all[:, bi, :], in_=ps[0:64, bo, 0:128]), ["ps%d" % bo], ["o_all"])
        dma(POOL, s_o.rearrange("b r d -> r b d"), o_all[:, :, :], ["o_all"], ["s_o"])
        for bi in range(16):
            for g in range(2):
                dma(POOL, o_tok[bi * 4:(bi + 1) * 4, g * 512:(g + 1) * 512].rearrange("t (h d) -> t h d", h=8),
                    s_o[bi, g * 32:(g + 1) * 32, g * 64:(g + 1) * 64].rearrange("(h t) d -> t h d", t=4), ["s_o"], ["o_tok_%d_%d" % (bi, g)])
        if dbg == "pm":
            out_dmas.append(dma(POOL, ys_d[:, 0:256], Pm[0:64, 1, 0, :], ["Pm1"], []))
            out_dmas.append(dma(POOL, ys_d[:, 256:384], o_all[:, 15, :], ["o_all"], []))
            out_dmas.append(dma(POOL, ys_d[:, 384:512], o_all[:, 0, :], ["o_all"], []))
            out_dmas.append(dma(POOL, ys_d[:, 512:536], att[0:64, 1].rearrange("p a b -> p (a b)"), ["att1"], []))
        OTK = ["o_tok_%d_%d" % (bi, g) for bi in range(16) for g in range(2)]
        if dbg == "otok":
            out_dmas.append(dma(POOL, ys_d, o_tok[:, :], OTK, []))
        if dbg == "qb":
            out_dmas.append(dma(POOL, ys_d, qb[0:64, 0, :], ["qb0"], []))
        bt = bank()
        for k in range(KC):
            op(PE, lambda e, k=k, bt=bt: e.transpose(psT(bt, 512)[:, k * 64:(k + 1) * 64], o_tok[:, k * 128:(k + 1) * 128], identb[0:64, 0:64]),
               OTK + ["c_identb"], ["ps%d" % bt])
        op(DVE, lambda e, bt=bt: e.tensor_copy(out=oT[:, 0, :, 0:64], in_=psT(bt, 512).rearrange("p (k t) -> p k t", k=KC)), ["ps%d" % bt], ["oT0"])
        for n in range(2):
            wv, wk = ow[n]
            b = bank()
            for k in range(KC):
                mm(ps[0:rn, b, :], oT[:, 0, k, 0:64], None if wv is None else wv[:, k, :], k == 0, k == KC - 1, ["oT0", wk], ["ps%d" % b])
            op(DVE, lambda e, b=b, n=n: e.tensor_tensor(out=x[0:rn, 0, n * 512:(n + 1) * 512], in0=ps[0:rn, b, :], in1=x[0:rn, 0, n * 512:(n + 1) * 512],
                                                       op=ALU.add), ["ps%d" % b, "x0"], ["x0"])

    def final_norm(ctx):
        rn, chunks = ctx["rn"], ctx["chunks"]
        for c in chunks:
            r = rms_stats(x[0:rn, XS(c), :], rn, D, [XK(c)])
            op(DVE, lambda e, c=c, r=r: e.scalar_tensor_tensor(out=yout[0:rn, 0, :], in0=x[0:rn, XS(c), :], scalar=stat[0:rn, r:r + 1],
                                                               in1=fnw[0:rn, :], op0=ALU.mult, op1=ALU.mult),
               [XK(c), "st%d" % r, "c_fnw"], TH)
            dst = ctx["y_dst"](c)
            if dbg is not None and ctx["sample"]:
                dst = None
            if dst is not None:
                out_dmas.append(dma(POOL, dst, yout[0:rn, 0, :], TH, []))

    def emit_conv_state(state_ap_fn, nrows, dst, key):
        for g6 in range(6):
            n4 = 4 if g6 < 5 else 2
            b = bank()
            for q in range(n4):
                jj = g6 * 4 + q
                op(PE, lambda e, b=b, q=q, jj=jj: e.transpose(ps[0:nrows, b, q * 128:(q + 1) * 128], state_ap_fn(jj), identf[:, :]),
                   [key, "c_identf"], ["ps%d" % b])
            op(DVE, lambda e, b=b, g6=g6, n4=n4: e.tensor_copy(out=tr_out[0:nrows, g6 * 512:g6 * 512 + n4 * 128], in_=ps[0:nrows, b, 0:n4 * 128]),
               ["ps%d" % b], K_F)
        out_dmas.append(dma(POOL, dst, tr_out[0:nrows, :], K_F, []))

    def prompt_ctx(t):
        ctx = dict(rn=128, chunks=CH4, ntok=512, nseq=1, L=512, sample=False, first=(t == 1), last=(t == NT - 1))
        ctx["p_src"] = lambda l, c, t=t: pp_d[l, (t * 4 + c) * 128:(t * 4 + c + 1) * 128, :]
        ctx["y_dst"] = (lambda c, t=t: yp_d[((t - 1) * 4 + c) * 128:((t - 1) * 4 + c + 1) * 128, :]) if t >= 1 else (lambda c: None)
        ctx1 = ctx
        if t == 0:
            ctx = dict(ctx, chunks=[1, 2, 3], ntok=384, L=384, t0=128)
            ctx1 = dict(ctx, chunks=[3], ntok=128, L=128, t0=384)
        return ctx, ctx1

    def prompt_loads(t, ctx):
        cur_tile[0] = t
        for c in ctx["chunks"]:
            if c == 0 and t > 0:
                continue
            dma(SP, x[:, XS(c), :], xp_d[(t * 4 + c) * 128:(t * 4 + c + 1) * 128, :], [], [XK(c)])
        c_lo = ctx["chunks"][0]
        dma(SP, rope_t[:, c_lo:4, :], ropep_d[t * 512 + c_lo * 128:(t + 1) * 512, :].rearrange("(c p) f -> p c f", p=128), [], ["rope_t"])

    def carry_kv():
        op(DVE, lambda e: e.tensor_copy(out=kTd[:, :, 0:128], in_=kTd[:, :, 512:640]), ["kT4"], ["kT0"])
        op(DVE, lambda e: e.tensor_copy(out=vtok[:, 0, :], in_=vtok[:, 4, :]), ["vt4"], ["vt0"])

    def prefetch_next_x(t):
        if t + 1 < n_tiles:
            ns_ = (4 * (t + 1)) % 5
            dma(SP, x[:, ns_, :], xp_d[((t + 1) * 4) * 128:((t + 1) * 4 + 1) * 128, :], [], ["x%d" % ns_])

    def sample_ctx():
        ctx = dict(rn=64, chunks=[0], ntok=64, nseq=16, L=4, sample=True, first=False, last=False)
        ctx["p_src"] = lambda l, c: pss_d[l]
        ctx["y_dst"] = lambda c: ys_d
        return ctx

    def sample_loads():
        dma(SP, x[0:64, 0, :], xs_d, [], ["x0"])
        dma(SP, rope_t[0:64, 0, :], ropes_d, [], ["rope_t"])
        for l in range(2):
            dma(SP, cstf, cst_d[l], [], K_F)
            for g6 in range(6):
                n4 = 4 if g6 < 5 else 2
                b = bank()
                for q in range(n4):
                    jj = g6 * 4 + q
                    op(PE, lambda e, b=b, q=q, jj=jj: e.transpose(ps[:, b, q * 32:(q + 1) * 32], cstf[:, jj * 128:(jj + 1) * 128], identf[0:32, 0:32]),
                       K_F + ["c_identf"], ["ps%d" % b])
                op(DVE, lambda e, b=b, g6=g6, n4=n4, l=l: e.tensor_copy(out=gsts[:, l, g6 * 4:g6 * 4 + n4, :],
                                                                      in_=ps[:, b, 0:n4 * 32].rearrange("p (q r) -> p q r", q=n4)),
                   ["ps%d" % b], ["gst%d" % l])

    def run_all():
        ctxH, ctxH1 = prompt_ctx(0)
        ctxS = sample_ctx()
        if do_sample:
            dma(POOL, ckb[:, :, :], ck_d.rearrange("b k f -> k b f"), [], ["ckb_all"])
            dma(POOL, cvb[:, :, :], cv_d.rearrange("b k f -> k b f"), [], ["cvb_all"])
        prompt_loads(0, ctxH)
        if do_sample:
            sample_loads()
        S = do_sample
        fence(K_A + K_F, K_U)
        fence(K_Q, K_VN)
        drive([gmlp(ctxH)] + ([gmlp(ctxS)] if S else []))
        fence(K_U, K_A)
        drive([ffn(ctxH, 0)] + ([ffn(ctxS, 0)] if S else []))
        if S:
            fence(K_A, K_F)
            emit_conv_state(lambda jj: gnew[:, jj, :], 32, ncs_d[0], "gnew")
        drive([ple(ctxH, 0)] + ([ple(ctxS, 0)] if S else []))
        carry_kv()
        drive([shared_kv(ctxH)] + ([shared_kv(ctxS)] if S else []))
        prefetch_next_x(0)
        fence(K_VN, K_Q)
        drive([attn_layer_prompt(ctxH1)] + ([attn_layer_sample(ctxS)] if S else []))
        fence(K_F, K_A)
        drive([ffn(ctxH1, 1)] + ([ffn(ctxS, 1)] if S else []))
        if S:
            fence(K_A, K_F)
            emit_conv_state(lambda jj: gnew[:, jj, :], 32, ncs_d[1], "gnew")
            drive([ple(ctxS, 1)])
            final_norm(ctxS)
        for t in range(1, n_tiles):
            ctx, ctx1 = prompt_ctx(t)
            prompt_loads(t, ctx)
            fence(K_A + K_F, K_U)
            fence(K_Q, K_VN)
            drive([gmlp(ctx)])
            fence(K_U, K_A)
            drive([ffn(ctx, 0)])
            drive([ple(ctx, 0)])
            carry_kv()
            drive([shared_kv(ctx)])
            prefetch_next_x(t)
            fence(K_VN, K_Q)
            drive([attn_layer_prompt(ctx1)])
            drive([ffn(ctx1, 1)])
            drive([ple(ctx1, 1)])
            final_norm(ctx1)
        fence(K_A, K_F)
        for l in range(2):
            emit_conv_state(lambda jj, l=l: gstp[:, l, jj, 0:2], 2, ncp_d[l], "gst%d" % l)

    run_all()
    ring.planning = False
    bank_ctr[0] = 0
    stat_ctr[0] = 0
    run_all()
    finals = [o for o in out_dmas if o is not None]
    P.emit({POOL: finals})
    st.close()
    return nc


_CACHE = {}


def _host_inputs(inp):
    f32 = np.float32
    x_prompt = np.asarray(inp["x_prompt"], f32)
    p_prompt = np.asarray(inp["p_prompt"], f32)
    x_sample = np.asarray(inp["x_sample"], f32)
    p_sample = np.asarray(inp["p_sample"], f32)
    state = np.asarray(inp["state_ffn_conv"], f32)
    ck = np.asarray(inp["cache_k_win"], f32)
    cv = np.asarray(inp["cache_v_win"], f32)
    norms = [inp["norm_mix"][0], inp["norm_ffn"][0], inp["ple_norm"][0], inp["kv_norm"], inp["norm_mix"][1], inp["norm_ffn"][1],
             inp["ple_norm"][1], inp["final_norm"]]
    nw_fm = np.stack([np.asarray(w, f32).reshape(KC, 128).T for w in norms], axis=1)
    convw = np.zeros((128, 2, NFC, 4), f32)
    cw = np.asarray(inp["ffn_conv_w"], f32)
    cbias = np.asarray(inp["ffn_conv_b"], f32)
    for l in range(2):
        for k in range(3):
            convw[:, l, :, k] = cw[l, k].reshape(NFC, 128).T
        convw[:, l, :, 3] = cbias[l].reshape(NFC, 128).T
    ws = np.asarray(inp["gm_w_s"], f32)[0]
    bs = np.asarray(inp["gm_b_s"], f32)[0]
    tri = np.tril(np.ones((128, 128), bool))
    wsT = np.where(tri[None], ws, 0).transpose(2, 0, 1).copy()
    wsTs = np.zeros((64, 8, 64), f32)
    tri4 = np.tril(np.ones((4, 4), bool))
    blk = np.where(tri4[None], ws[:, :4, :4], 0).transpose(2, 0, 1)
    for b in range(16):
        wsTs[b * 4:(b + 1) * 4, :, b * 4:(b + 1) * 4] = blk
    bsp = bs[None, :, :].copy()
    bss = np.tile(bs[:, :4], (1, 16))[None].copy()
    sinks = np.asarray(inp["attn_sinks"], f32).reshape(1, 16)
    sinkrow = np.repeat(sinks[0], 4).reshape(64, 1).copy()
    half = 8
    inv_freq = (500000.0 ** (-np.arange(half, dtype=np.float32) / half)).astype(f32)
    tq = np.arange(128)[:, None]
    tk = np.arange(128)[None, :]
    mask_a = np.full((128, 256), NEG, f32)
    mask_a[:, 0:128][tk > tq] = 0.0
    mask_a[:, 128:256][tk <= tq] = 0.0
    mask_first = mask_a.copy()
    mask_first[:, 0:128] = NEG
    mask_s = np.full((64, 132), NEG, f32)
    for r in range(64):
        qi = r % 4
        sj = np.arange(132)
        mask_s[r, (sj > qi) & (sj <= qi + 128)] = 0.0
    ident = np.eye(128, dtype=f32)
    common = {
        "gm_w_in": np.ascontiguousarray(np.asarray(inp["gm_w_in"], f32)[0]),
        "gm_w_out": np.ascontiguousarray(np.asarray(inp["gm_w_out"], f32)[0]),
        "w_kv": np.asarray(inp["w_kv"], f32),
        "w_q": np.ascontiguousarray(np.asarray(inp["w_q"], f32)[0]),
        "w_o": np.ascontiguousarray(np.asarray(inp["w_o"], f32)[0]),
        "ffn_w_gate": np.asarray(inp["ffn_w_gate"], f32),
        "ffn_w_up": np.asarray(inp["ffn_w_up"], f32),
        "ffn_w_down": np.asarray(inp["ffn_w_down"], f32),
        "ple_w_gate": np.asarray(inp["ple_w_gate"], f32),
        "ple_w_proj": np.asarray(inp["ple_w_proj"], f32),
        "nw_fm": np.ascontiguousarray(nw_fm), "fnw": np.asarray(inp["final_norm"], f32).reshape(1, D),
        "vnw": np.asarray(inp["gm_v_norm"], f32).reshape(1, GH), "convw": convw,
        "wsT": wsT.astype(f32), "wsTs": wsTs, "bsp": bsp, "bss": bss, "sinks": sinks, "sinkrow": sinkrow,
        "ident": ident, "mask_a": mask_a, "mask_s": mask_s,
    }
    pos_s = (16384 + np.arange(4)).astype(f32)
    ang_s = pos_s[:, None] * inv_freq[None, :]
    rope_s = np.tile(np.concatenate([np.cos(ang_s), np.sin(ang_s)], axis=1).astype(f32), (16, 1))
    maps = []
    for core in range(8):
        b, hf = core // 2, core % 2
        lo = hf * 4096 - 512
        xp = np.zeros((NCHUNK_CORE * 128, D), f32)
        pp = np.zeros((2, NCHUNK_CORE * 128, PLE), f32)
        if hf == 0:
            xp[512:] = x_prompt[b, 0:4096]
            pp[:, 512:] = p_prompt[:, b, 0:4096]
        else:
            xp[:] = x_prompt[b, lo:lo + 4608]
            pp[:] = p_prompt[:, b, lo:lo + 4608]
        pos = (lo + np.arange(NCHUNK_CORE * 128)).astype(f32)
        ang = pos[:, None] * inv_freq[None, :]
        rope_p = np.concatenate([np.cos(ang), np.sin(ang)], axis=1).astype(f32)
        m = dict(common)
        m.update({
            "xp": xp, "pp": pp,
            "xs": np.ascontiguousarray(x_sample[core * 16:(core + 1) * 16].reshape(64, D)),
            "pss": np.ascontiguousarray(p_sample[:, core * 16:(core + 1) * 16].reshape(2, 64, PLE)),
            "cst": np.ascontiguousarray(state[:, core * 16:(core + 1) * 16].reshape(2, 32, FF)),
            "ck": np.ascontiguousarray(ck[core * 16:(core + 1) * 16].reshape(16, 128, 128)),
            "cv": np.ascontiguousarray(cv[core * 16:(core + 1) * 16].reshape(16, 128, 128)),
            "rope_p": rope_p, "rope_s": rope_s,
            "mask_b": mask_first if hf == 0 else mask_a,
        })
        maps.append(m)
    return maps


def kernel(**inputs):
    if "nc" not in _CACHE:
        _CACHE["nc"] = build_program()
    nc = _CACHE["nc"]
    maps = _host_inputs(inputs)
    res = run_bass_kernel_spmd(nc, maps, core_ids=list(range(8)))
    R = res.results
    f32 = np.float32
    y_prompt = np.zeros((4, 8192, D), f32)
    y_sample = np.zeros((128, 4, D), f32)
    gmv = np.zeros((1, 128, 4, GH), f32)
    ncp = np.zeros((2, 4, 2, FF), f32)
    ncs = np.zeros((2, 128, 2, FF), f32)
    nkp = np.zeros((4, 128, 2, 64), f32)
    nvp = np.zeros((4, 128, 2, 64), f32)
    nks = np.zeros((128, 4, 2, 64), f32)
    nvs = np.zeros((128, 4, 2, 64), f32)
    for core in range(8):
        b, hf = core // 2, core % 2
        r = R[core]
        y_prompt[b, hf * 4096:(hf + 1) * 4096] = r["yp"]
        sl = slice(core * 16, (core + 1) * 16)
        y_sample[sl] = r["ys"].reshape(16, 4, D)
        gmv[0, sl] = r["gmv"].reshape(16, 4, GH)
        ncs[:, sl] = r["ncs"].reshape(2, 16, 2, FF)
        nks[sl] = r["nks"].reshape(16, 4, 2, 64)
        nvs[sl] = r["nvs"].reshape(16, 4, 2, 64)
        if hf == 1:
            ncp[:, b] = r["ncp"]
            nkp[b] = r["nkp"].reshape(128, 2, 64)
            nvp[b] = r["nvp"].reshape(128, 2, 64)
    return (y_prompt, y_sample, gmv, ncp, ncs, nkp, nvp, nks, nvs)
```

```python
import contextlib
import numpy as np
import concourse.bass as bass
import concourse.mybir as mybir
from concourse.bass_utils import run_bass_kernel_spmd

F32 = mybir.dt.float32
BF16 = mybir.dt.bfloat16
I32 = mybir.dt.int32
AF = mybir.ActivationFunctionType
ALU = mybir.AluOpType
AX = mybir.AxisListType

PE, ACT, DVE, POOL, SP = "pe", "act", "dve", "pool", "sp"
ENGS = (PE, ACT, DVE, POOL, SP)

D = 1024
KC = 8
GH = 2048
FF = 2816
NFC = 22
PLE = 256
NCHUNK_CORE = 36
NT = 9
NEG = -30000.0
EPS = 1e-6
SLOT = 5632
NS = 4


class Op:
    __slots__ = ("eng", "fn", "deps", "is_dma", "signal", "tok", "dma_prev", "cost", "idx", "pos", "fin")

    def __init__(self, eng, fn, is_dma, cost):
        self.eng = eng
        self.fn = fn
        self.deps = []
        self.is_dma = is_dma
        self.signal = False
        self.tok = None
        self.dma_prev = None
        self.cost = cost
        self.idx = 0
        self.pos = 0
        self.fin = None


DEFAULT_COST = {PE: 200.0, ACT: 600.0, DVE: 600.0, POOL: 400.0, SP: 100.0}
SCHED_WINDOW = {PE: 256, ACT: 48, DVE: 48, POOL: 16, SP: 16}


class Prog:
    def __init__(self, nc, n_dma_sems=12, schedule=True):
        self.nc = nc
        self.ops = {e: [] for e in ENGS}
        self.last_w = {}
        self.readers = {}
        self.n_dma_sems = n_dma_sems
        self.n = 0
        self.do_schedule = schedule

    def op(self, eng, fn, reads=(), writes=(), dma=False, cost=None):
        if cost is None:
            cost = 4000.0 if dma else DEFAULT_COST[eng]
        o = Op(eng, fn, dma, cost)
        o.idx = self.n
        self.n += 1
        deps = []
        for k in reads:
            w = self.last_w.get(k)
            if w is not None:
                deps.append(w)
        for k in writes:
            w = self.last_w.get(k)
            if w is not None:
                deps.append(w)
            rs = self.readers.get(k)
            if rs:
                deps.extend(rs)
        seen = set()
        for d in deps:
            if id(d) in seen:
                continue
            seen.add(id(d))
            o.deps.append(d)
        for k in writes:
            self.last_w[k] = o
            self.readers[k] = []
        ws = set(writes)
        for k in reads:
            if k in ws:
                continue
            self.readers.setdefault(k, []).append(o)
        self.ops[eng].append(o)
        return o

    def schedule(self):
        pend = {e: list(self.ops[e]) for e in ENGS}
        head = {e: 0 for e in ENGS}
        done_flag = {}
        tfree = {e: 0.0 for e in ENGS}
        order = {e: [] for e in ENGS}
        remaining = sum(len(v) for v in pend.values())
        while remaining:
            progressed = False
            for e in sorted(ENGS, key=lambda q: tfree[q]):
                lst = pend[e]
                h = head[e]
                while h < len(lst) and lst[h] is None:
                    h += 1
                head[e] = h
                if h >= len(lst):
                    continue
                best = None
                best_start = None
                cnt = 0
                i = h
                W = SCHED_WINDOW[e]
                while i < len(lst) and cnt < W:
                    o = lst[i]
                    if o is not None:
                        cnt += 1
                        ready = 0.0
                        ok = True
                        for d in o.deps:
                            f = d.fin
                            if f is None:
                                ok = False
                                break
                            if f > ready:
                                ready = f
                        if ok:
                            st_ = ready if ready > tfree[e] else tfree[e]
                            if best is None or st_ < best_start - 1e-9:
                                best, best_start, best_i = o, st_, i
                                if st_ <= tfree[e]:
                                    break
                    i += 1
                if best is None:
                    continue
                lst[best_i] = None
                if best.is_dma:
                    tfree[e] = best_start + 60.0
                    best.fin = best_start + best.cost
                else:
                    best.fin = best_start + best.cost
                    tfree[e] = best.fin
                order[e].append(best)
                remaining -= 1
                progressed = True
                break
            assert progressed, "scheduler deadlock"
        self.ops = order
        return max(tfree.values())

    def emit(self, final_waits):
        nc = self.nc
        if self.do_schedule:
            self.est_ns = self.schedule()
        for e in ENGS:
            for i, o in enumerate(self.ops[e]):
                o.pos = i
        for e in ENGS:
            for o in self.ops[e]:
                latest = {}
                nd = []
                for d in o.deps:
                    if d.is_dma:
                        nd.append(d)
                        continue
                    if d.eng == e and e == PE and not o.is_dma:
                        continue
                    cur = latest.get(d.eng)
                    if cur is None or d.pos > cur.pos:
                        latest[d.eng] = d
                nd.extend(latest.values())
                o.deps = nd
                for d in nd:
                    d.signal = True
        with contextlib.ExitStack() as st:
            esem = {e: st.enter_context(nc.semaphore("s_" + e)) for e in ENGS}
            dsem = {}
            for e in (SP, ACT, POOL):
                if any(o.is_dma for o in self.ops[e]):
                    dsem[e] = [st.enter_context(nc.semaphore("d_%s_%d" % (e, i))) for i in range(self.n_dma_sems)]
            for e in ENGS:
                cnt = 0
                dcnt = 0
                uses = [0] * self.n_dma_sems
                for o in self.ops[e]:
                    if o.is_dma:
                        si = dcnt % self.n_dma_sems
                        dcnt += 1
                        prev = uses[si]
                        uses[si] += 16
                        o.tok = (dsem[e][si], uses[si])
                        o.dma_prev = (dsem[e][si], prev)
                    elif o.signal:
                        cnt += 1
                        o.tok = (esem[e], cnt)
            block = st.enter_context(nc.Block())
            engobj = {PE: block.tensor, ACT: block.scalar, DVE: block.vector, POOL: block.gpsimd, SP: block.sync}

            def make(e):
                def body(eng):
                    waited = {}

                    def wait(tok):
                        sem, val = tok
                        if val <= 0:
                            return
                        key = id(sem)
                        if waited.get(key, 0) >= val:
                            return
                        eng.wait_ge(sem, val)
                        waited[key] = val

                    for o in self.ops[e]:
                        for d in o.deps:
                            wait(d.tok)
                        if o.is_dma:
                            wait(o.dma_prev)
                            o.fn(eng).then_inc(o.tok[0], 16)
                        else:
                            ins = o.fn(eng)
                            if o.signal:
                                ins.then_inc(o.tok[0], 1)
                    for fo in final_waits.get(e, ()):
                        wait(fo.tok)
                return body

            for e in ENGS:
                if self.ops[e] or final_waits.get(e):
                    engobj[e](make(e))


def build_program(n_tiles=NT, do_sample=True, stage=99, dbg=None):
    nc = bass.Bass("TRN2", target_bir_lowering=False)

    def din(name, shape, dt=F32):
        return nc.dram_tensor(name, list(shape), dt, kind="ExternalInput").ap()

    def dout(name, shape, dt=F32):
        return nc.dram_tensor(name, list(shape), dt, kind="ExternalOutput").ap()

    def dscr(name, shape, dt=BF16):
        return nc.dram_tensor(name, list(shape), dt, kind="Internal").ap()

    NROW = NCHUNK_CORE * 128
    xp_d = din("xp", [NROW, D])
    pp_d = din("pp", [2, NROW, PLE])
    xs_d = din("xs", [64, D])
    pss_d = din("pss", [2, 64, PLE])
    cst_d = din("cst", [2, 32, FF])
    ck_d = din("ck", [16, 128, 128])
    cv_d = din("cv", [16, 128, 128])
    w_in_d = din("gm_w_in", [D, 2 * GH])
    w_out_d = din("gm_w_out", [GH, D])
    w_kv_d = din("w_kv", [D, 256])
    w_q_d = din("w_q", [D, D])
    w_o_d = din("w_o", [D, D])
    w_gate_d = din("ffn_w_gate", [2, D, FF])
    w_up_d = din("ffn_w_up", [2, D, FF])
    w_down_d = din("ffn_w_down", [2, FF, D])
    w_pg_d = din("ple_w_gate", [2, D, D])
    w_pp_d = din("ple_w_proj", [2, PLE, D])
    nwfm_d = din("nw_fm", [128, 8, KC])
    fnw_d = din("fnw", [1, D])
    vnw_d = din("vnw", [1, GH])
    convw_d = din("convw", [128, 2, NFC, 4])
    wsT_d = din("wsT", [128, 8, 128])
    wsTs_d = din("wsTs", [64, 8, 64])
    bsp_d = din("bsp", [1, 8, 128])
    bss_d = din("bss", [1, 8, 64])
    sinks_d = din("sinks", [1, 16])
    sinkrow_d = din("sinkrow", [64, 1])
    ropep_d = din("rope_p", [NROW, 16])
    ropes_d = din("rope_s", [64, 16])
    ident_d = din("ident", [128, 128])
    maska_d = din("mask_a", [128, 256])
    maskb_d = din("mask_b", [128, 256])
    masks_d = din("mask_s", [64, 132])

    yp_d = dout("yp", [32 * 128, D])
    ys_d = dout("ys", [64, D])
    gmv_d = dout("gmv", [64, GH])
    ncp_d = dout("ncp", [2, 2, FF])
    ncs_d = dout("ncs", [2, 32, FF])
    nkp_d = dout("nkp", [128, 128])
    nvp_d = dout("nvp", [128, 128])
    nks_d = dout("nks", [64, 128])
    nvs_d = dout("nvs", [64, 128])

    s_w_in = dscr("s_w_in", [D, 2 * GH])
    s_w_out = dscr("s_w_out", [GH, D])
    s_w_kv = dscr("s_w_kv", [D, 256])
    s_w_q = dscr("s_w_q", [D, D])
    s_w_o = dscr("s_w_o", [D, D])
    s_w_gate = dscr("s_w_gate", [2, D, FF])
    s_w_up = dscr("s_w_up", [2, D, FF])
    s_w_down = dscr("s_w_down", [2, FF, D])
    s_w_pg = dscr("s_w_pg", [2, D, D])
    s_w_pp = dscr("s_w_pp", [2, PLE, D])
    s_kv = dscr("s_kvs", [64, 256])
    s_o = dscr("s_os", [16, 64, 128])

    P = Prog(nc)
    st = contextlib.ExitStack()

    def sb(name, shape, dt):
        return st.enter_context(nc.sbuf_tensor(name, list(shape), dt))

    ps = st.enter_context(nc.psum_tensor("ps", [128, 8, 512], F32))
    psb = ps[:].bitcast(BF16) if False else None

    wring = sb("wring", [128, NS, SLOT], BF16)
    x = sb("x", [128, 5, D], F32)
    cur_tile = [0]

    def XS(c):
        return (4 * cur_tile[0] + c) % 5

    def XK(c):
        return "x%d" % XS(c)
    hT = sb("hT", [128, KC, 512], BF16)
    big = sb("big", [128, NFC, 512], BF16)
    vn = sb("vn", [128, 4, GH], BF16)
    vnflat = vn[:].rearrange("p c f -> p (c f)")
    qT = vnflat[:, 0:4096].rearrange("p (k t) -> p k t", k=KC)
    qb = vnflat[:, 4096:6144].rearrange("p (a f) -> p a f", a=2)
    Pm = vnflat[:, 6144:8192].rearrange("p (a h k) -> p a h k", a=2, h=4)
    htok = sb("htok", [128, 2, D], BF16)
    cb = sb("cb", [128, 2, 512], F32)
    ge = sb("ge", [128, 2, 512], BF16)
    junk = ge[:].rearrange("p a f -> p (a f)")
    GEK = ["ge0", "ge1"]
    gp = sb("gp", [128, 2, 516], F32)
    ub = sb("ub", [128, 2, 512], BF16)
    pf = sb("pf", [128, 2, PLE], F32)
    pb = sb("pb", [128, 2, PLE], BF16)
    pT = sb("pT", [128, 2, 512], BF16)
    th = sb("th", [128, 2, 512], F32)
    kvf = sb("kvf", [128, 2, 256], F32)
    kb2 = sb("kb2", [128, 2, 256], BF16)
    kTd = sb("kTd", [128, 2, 5 * 128], BF16)
    vtok = sb("vtok", [128, 5, 128], BF16)
    Dg = sb("Dg", [128, 2, 4, 128], BF16)
    PT = sb("PT", [128, 2, 4, 2, 128], BF16)
    oT = sb("oT", [128, 2, KC, 128], BF16)
    rope_t = sb("rope_t", [128, 4, 16], F32)
    rt = sb("rt", [128, 4, 16, 8], F32)
    stat = sb("stat", [128, 64], F32)
    att = sb("att", [128, 2, 6, 4], F32)
    gstp = sb("gstp", [128, 2, NFC, 2], F32)
    gsts = sb("gsts", [128, 2, NFC, 32], F32)
    gnew = sb("gnew", [128, NFC, 32], F32)
    nwfm = sb("nwfm", [128, 8, KC], F32)
    fnw = sb("fnw_sb", [128, D], F32)
    vnw = sb("vnw_sb", [128, GH], F32)
    convw = sb("convw_sb", [128, 2, NFC, 4], F32)
    thflat = th[:].rearrange("p a f -> p (a f)")
    yout = th[:].rearrange("p a f -> p (a f)").unsqueeze(1)
    wsTf = thflat[:, 0:1024].rearrange("p (g t) -> p g t", g=8)
    wsT = sb("wsT_sb", [128, 8, 128], BF16)
    wsTsf = thflat[0:64, 0:512].rearrange("p (g t) -> p g t", g=8)
    wsTs = sb("wsTs_sb", [64, 8, 64], BF16)
    bspf = thflat[0:1, 0:1024].rearrange("p (g t) -> p g t", g=8)
    bsp = sb("bsp_sb", [1, 8, 128], BF16)
    bssf = thflat[0:1, 0:512].rearrange("p (g t) -> p g t", g=8)
    bss = sb("bss_sb", [1, 8, 64], BF16)
    ones = sb("ones", [1, 128], BF16)
    sink8 = sb("sink8", [128, 16], F32)
    sinkr = sb("sinkr", [64, 2], F32)
    identf = sb("identf", [128, 128], F32)
    identb = sb("identb", [128, 128], BF16)
    maskf = thflat[:, 0:256]
    maska = sb("maska", [128, 256], BF16)
    maskb = sb("maskb", [128, 256], BF16)
    masks = sb("masks", [64, 132], BF16)
    nhalf = sb("nhalf", [128, 1], F32)
    bigF = big[:].rearrange("p j t -> p (j t)").bitcast(F32)
    tr_out = bigF[0:32, 0:FF]
    cstf = bigF[0:32, 0:FF]
    ckb = sb("ckb", [128, 16, 128], BF16)
    cvb = sb("cvb", [128, 16, 128], BF16)
    kTc = sb("kTc", [128, 2, 128], BF16)
    knew4 = sb("knew4", [4, 16, 128], BF16)
    kvb = sb("kvb", [64, 256], BF16)
    kTn = sb("kTn", [128, 64], BF16)
    qT2 = sb("qT2", [128, 16, 64], BF16)
    PTs = sb("PTs", [128, 2, 64], BF16)
    PTn = sb("PTn", [4, 2, 64], BF16)
    o_all = sb("o_all", [64, 16, 128], BF16)
    qtmp = sb("qtmp", [128, 512], BF16)
    o_tok = sb("o_tok", [64, D], BF16)

    print('SBUF_REMAIN', nc.sbuf_bytes_remaining)
    bank_ctr = [0]

    def bank(n=1):
        b = bank_ctr[0]
        if n == 2 and b % 2 == 1:
            b = (b + 1) % 8
        bank_ctr[0] = (b + n) % 8
        return b

    stat_ctr = [0]

    def scol(n=1):
        c = stat_ctr[0]
        if c + n > 64:
            c = 0
        stat_ctr[0] = c + n
        return c

    def load(eng, dst, src, key, reads=()):
        return P.op(eng, lambda e: e.dma_start(out=dst, in_=src), reads=list(reads), writes=[key], dma=True)

    TH = ["th0", "th1"]
    load(SP, nwfm[:], nwfm_d, "c_nwfm")
    load(SP, fnw[:], fnw_d.partition_broadcast(128), "c_fnw")
    load(SP, vnw[:], vnw_d.partition_broadcast(128), "c_vnw")
    load(SP, convw[:], convw_d, "c_convw")
    load(SP, sink8[:], sinks_d.partition_broadcast(128), "c_sink8")
    load(SP, sinkr[:, 0:1], sinkrow_d, "c_sinkr")
    load(SP, identf[:], ident_d, "c_identf")
    P.op(DVE, lambda e: e.tensor_copy(out=identb[:], in_=identf[:]), reads=["c_identf"], writes=["c_identb"])
    for (src_d, stg, dstt, kk) in ((wsT_d, wsTf, wsT, "c_wsT"), (wsTs_d, wsTsf, wsTs, "c_wsTs"), (bsp_d, bspf, bsp, "c_bsp"),
                                   (bss_d, bssf, bss, "c_bss"), (maska_d, maskf, maska, "c_maska"), (maskb_d, maskf, maskb, "c_maskb"),
                                   (masks_d, thflat[0:64, 0:132], masks, "c_masks")):
        P.op(SP, lambda e, stg=stg, src_d=src_d: e.dma_start(out=stg, in_=src_d), writes=TH, dma=True)
        P.op(DVE, lambda e, stg=stg, dstt=dstt: e.tensor_copy(out=dstt[:], in_=stg), reads=TH, writes=[kk])
    P.op(DVE, lambda e: e.memset(ones[:], 1.0), writes=["c_ones"])
    P.op(DVE, lambda e: e.memset(nhalf[:], -0.5), writes=["c_nhalf"])
    P.op(DVE, lambda e: e.tensor_scalar(out=sink8[:], in0=sink8[:], scalar1=8.0, scalar2=None, op0=ALU.mult),
         reads=["c_sink8"], writes=["c_sink8"])
    P.op(DVE, lambda e: e.tensor_scalar(out=sinkr[:, 1:2], in0=sinkr[:, 0:1], scalar1=8.0, scalar2=None, op0=ALU.mult),
         reads=["c_sinkr"], writes=["c_sinkr8"])
    P.op(DVE, lambda e: e.memset(gstp[:], 0.0), writes=["gst0", "gst1"])
    P.op(DVE, lambda e: e.memset(kTd[:], 0.0), writes=["kT0", "kT1", "kT2", "kT3", "kT4"])
    P.op(DVE, lambda e: e.memset(vtok[:], 0.0), writes=["vt0", "vt1", "vt2", "vt3", "vt4"])

    cast_list = []
    cast_done = set()

    def cast(dst, src, key):
        cast_list.append((dst, src, key))

    for i in range(4):
        cast(s_w_in[i * 256:(i + 1) * 256, :], w_in_d[i * 256:(i + 1) * 256, :], "S_w_in%d" % i)
    for i in range(2):
        cast(s_w_out[i * 1024:(i + 1) * 1024, :], w_out_d[i * 1024:(i + 1) * 1024, :], "S_w_out%d" % i)
    for l in range(2):
        if l == 1:
            cast(s_w_q, w_q_d, "S_w_q")
            cast(s_w_o, w_o_d, "S_w_o")
        for i in range(2):
            cast(s_w_gate[l, i * 512:(i + 1) * 512, :], w_gate_d[l, i * 512:(i + 1) * 512, :], "S_w_gate%d_%d" % (l, i))
            cast(s_w_up[l, i * 512:(i + 1) * 512, :], w_up_d[l, i * 512:(i + 1) * 512, :], "S_w_up%d_%d" % (l, i))
        for i in range(2):
            cast(s_w_down[l, i * 1408:(i + 1) * 1408, :], w_down_d[l, i * 1408:(i + 1) * 1408, :], "S_w_down%d_%d" % (l, i))
        cast(s_w_pg[l], w_pg_d[l], "S_w_pg%d" % l)
        cast(s_w_pp[l], w_pp_d[l], "S_w_pp%d" % l)
        if l == 0:
            cast(s_w_kv, w_kv_d, "S_w_kv")

    def ensure_cast(keys):
        need = [k for k in keys if k not in cast_done]
        if not need:
            return
        last = max(i for i, (_, _, k) in enumerate(cast_list) if k in need)
        for i in range(last + 1):
            dst, src, k = cast_list[i]
            if k in cast_done:
                continue
            cast_done.add(k)
            P.op(POOL, lambda e, dst=dst, src=src: e.dma_start(out=dst, in_=src), writes=[k], dma=True, cost=40000.0)

    SK = {
        "in": ["S_w_in%d" % i for i in range(4)], "out": ["S_w_out0", "S_w_out1"], "kv": ["S_w_kv"], "q": ["S_w_q"], "o": ["S_w_o"],
        "gate0": ["S_w_gate0_0", "S_w_gate0_1"], "gate1": ["S_w_gate1_0", "S_w_gate1_1"],
        "up0": ["S_w_up0_0", "S_w_up0_1"], "up1": ["S_w_up1_0", "S_w_up1_1"],
        "down0": ["S_w_down0_0", "S_w_down0_1"], "down1": ["S_w_down1_0", "S_w_down1_1"],
        "pg0": ["S_w_pg0"], "pg1": ["S_w_pg1"], "pp0": ["S_w_pp0"], "pp1": ["S_w_pp1"],
    }

    class Ring:
        def __init__(self):
            self.plan = []
            self.planning = True
            self.i = 0
            self.issued = 0
            self.seen = set()

        def _issue(self, j):
            src, kc, ncol, skeys, f32, bid = self.plan[j]
            s = j % NS
            dst = wring[:, s, 0:kc * ncol].rearrange("p (k n) -> p k n", k=kc)
            bkey = "SB_%s_%s" % (bid[0], bid[1])
            nbytes = kc * ncol * 256
            if bid not in self.seen:
                self.seen.add(bid)
                P.op(POOL, lambda e: e.dma_start(out=dst, in_=f32), writes=["w%d" % s], dma=True, cost=2500.0 + 2 * nbytes / 150.0)
                P.op(SP, lambda e: e.dma_start(out=src, in_=dst), reads=["w%d" % s], writes=[bkey], dma=True, cost=2500.0 + nbytes / 150.0)
            else:
                P.op(SP, lambda e: e.dma_start(out=dst, in_=src), reads=[bkey], writes=["w%d" % s], dma=True,
                     cost=2000.0 + nbytes / 180.0)

        def next(self, src, kc, ncol, skeys, live=1, f32=None, bid=None):
            assert kc * ncol <= SLOT
            if self.planning:
                self.plan.append((src, kc, ncol, skeys, f32, bid))
                return None, None
            j = self.i
            self.i += 1
            while self.issued < min(len(self.plan), j + NS - live + 1):
                self._issue(self.issued)
                self.issued += 1
            assert self.issued > j
            s = j % NS
            return wring[:, s, 0:kc * ncol].rearrange("p (k n) -> p k n", k=kc), "w%d" % s

    ring = Ring()

    def REQ(*a, **k):
        return (a, k)

    def drive(gens):
        cur = []
        for g in gens:
            try:
                cur.append(next(g))
            except StopIteration:
                cur.append(None)
        while any(c_ is not None for c_ in cur):
            req = next(c_ for c_ in cur if c_ is not None)["a"][0]
            res = ring.next(*req[0], **req[1])
            for i_, g in enumerate(gens):
                if cur[i_] is None:
                    continue
                assert cur[i_]["a"][0][1]["bid"] == req[1]["bid"], (cur[i_]["a"][0][1]["bid"], req[1]["bid"])
                try:
                    cur[i_] = g.send(res)
                except StopIteration:
                    cur[i_] = None

    def wview(scr2d, c0, ncol):
        return scr2d.rearrange("(k p) n -> p k n", p=128)[:, :, c0:c0 + ncol]

    def mm(out, lhsT, rhs, start, stop, reads, writes):
        if ring.planning:
            return
        n = rhs.shape[-1]
        P.op(PE, lambda e: e.matmul(out, lhsT, rhs, start=start, stop=stop), reads=reads, writes=writes,
             cost=max(60.0, 16.0 + n / 2.2))

    def op(eng, fn, reads, writes, n=None):
        if ring.planning:
            return
        cost = None
        if eng == PE:
            cost = 70.0
        elif n is not None:
            cost = 100.0 + 1.15 * n
        t_rec = cur_tile[0]

        def fn2(e, fn=fn, t_rec=t_rec):
            old = cur_tile[0]
            cur_tile[0] = t_rec
            try:
                return fn(e)
            finally:
                cur_tile[0] = old
        P.op(eng, fn2, reads=reads, writes=writes, cost=cost)

    def dma(eng, dst, src, reads, writes, cost=None):
        if ring.planning:
            return None
        return P.op(eng, lambda e: e.dma_start(out=dst, in_=src), reads=reads, writes=writes, dma=True, cost=cost)

    out_dmas = []

    def fence(old, new):
        if ring.planning:
            return
        acc = []
        for k in old:
            w = P.last_w.get(k)
            if w is not None:
                acc.append(w)
            acc.extend(P.readers.get(k, []))
        uniq = []
        seen = set()
        for a in acc:
            if id(a) not in seen:
                seen.add(id(a))
                uniq.append(a)
        for k in new:
            P.readers.setdefault(k, []).extend(uniq)

    CH4 = [0, 1, 2, 3]
    K_U = ["uT%d_%d" % (q, c) for q in range(4) for c in CH4]
    K_A = ["aT%d_%d" % (j, c) for j in range(NFC) for c in CH4]
    K_F = ["bigF"]
    K_VN = ["vn%d" % c for c in CH4]
    K_Q = ["qT%d" % c for c in CH4] + ["qb0", "qb1", "Pm0", "Pm1"]

    def psT(b, ncols):
        return ps[:, b, :].bitcast(BF16)[:, 0:ncols]

    def rstd_from(c, rn, width):
        op(DVE, lambda e: e.tensor_scalar(out=stat[0:rn, c + 1:c + 2], in0=stat[0:rn, c:c + 1], scalar1=1.0 / width, scalar2=EPS,
                                          op0=ALU.mult, op1=ALU.add), ["st%d" % c], ["st%d" % (c + 1)], n=1)
        op(POOL, lambda e: e.tensor_tensor(out=stat[0:rn, c + 2:c + 3], in0=stat[0:rn, c + 1:c + 2], in1=nhalf[0:rn, :], op=ALU.pow),
           ["st%d" % (c + 1), "c_nhalf"], ["st%d" % (c + 2)], n=100)
        return c + 2

    def rms_stats(src_ap, rn, width, reads):
        c = scol(3)
        op(ACT, lambda e: e.activation(out=junk[0:rn, 0:width], in_=src_ap, func=AF.Square, accum_out=stat[0:rn, c:c + 1]),
           reads, GEK + ["st%d" % c])
        return rstd_from(c, rn, width)

    def norm_to_hT(ctx, nidx):
        rn = ctx["rn"]
        for c in ctx["chunks"]:
            r = rms_stats(x[0:rn, XS(c), :], rn, D, [XK(c)])
            hb = c % 2
            op(ACT, lambda e, c=c, r=r, hb=hb: e.activation(out=htok[0:rn, hb, :], in_=x[0:rn, XS(c), :], func=AF.Copy,
                                                           scale=stat[0:rn, r:r + 1]), [XK(c), "st%d" % r], ["htok%d" % hb], n=700)
            b = bank()
            for k in range(KC):
                op(PE, lambda e, k=k, b=b, hb=hb: e.transpose(psT(b, 1024)[:, k * rn:(k + 1) * rn], htok[0:rn, hb, k * 128:(k + 1) * 128],
                                                              identb[0:rn, 0:rn]), ["htok%d" % hb, "c_identb"], ["ps%d" % b])
            op(DVE, lambda e, b=b, c=c: e.tensor_tensor(
                out=hT[:, :, c * rn:(c + 1) * rn],
                in0=psT(b, KC * rn).rearrange("p (k t) -> p k t", k=KC),
                in1=nwfm[:, nidx, :].unsqueeze(2).to_broadcast([128, KC, rn]), op=ALU.mult),
               ["ps%d" % b, "c_nwfm"], ["hT%d" % c])

    def hT_keys(ctx):
        return ["hT%d" % c for c in ctx["chunks"]]

    def gmlp(ctx):
        rn, chunks, ntok = ctx["rn"], ctx["chunks"], ctx["ntok"]
        tk0 = ctx.get("t0", 0)
        sample = ctx["sample"]
        norm_to_hT(ctx, 0)
        sc = {c: scol(4) for c in chunks}
        def vfs(i):
            return (x[0:64, 4, i * 512:(i + 1) * 512], ["x4"]) if i < 2 else (thflat[0:64, (i - 2) * 512:(i - 1) * 512], TH)

        def vfs2(hf):
            return (x[0:64, 4, :], ["x4"]) if hf == 0 else (thflat[0:64, :], TH)
        for i in range(4):
            wv, wk = yield dict(a=(REQ(wview(s_w_in, GH + i * 512, 512), KC, 512, SK["in"], f32=wview(w_in_d, GH + i * 512, 512), bid=("in", GH + i * 512)),))
            for c in chunks:
                b = bank()
                for k in range(KC):
                    mm(ps[0:rn, b, :], hT[:, k, c * rn:(c + 1) * rn], None if wv is None else wv[:, k, :], k == 0, k == KC - 1,
                       ["hT%d" % c, wk], ["ps%d" % b])
                col = i * 512
                cc = sc[c] + i
                if sample:
                    vap, vk = vfs(i)
                    op(ACT, lambda e, b=b, vap=vap: e.activation(out=vap, in_=ps[0:rn, b, :], func=AF.Gelu_apprx_tanh),
                       ["ps%d" % b], vk)
                    op(ACT, lambda e, vap=vap, cc=cc: e.activation(out=junk[0:rn, 0:512], in_=vap, func=AF.Square,
                                                                   accum_out=stat[0:rn, cc:cc + 1]), vk, GEK + ["st%d" % cc])
                else:
                    op(ACT, lambda e, b=b, col=col, c=c: e.activation(out=vn[0:rn, c, col:col + 512], in_=ps[0:rn, b, :], func=AF.Gelu_apprx_tanh),
                       ["ps%d" % b], ["vn%d" % c])
                    op(ACT, lambda e, col=col, cc=cc, c=c: e.activation(out=junk[0:rn, 0:512], in_=vn[0:rn, c, col:col + 512], func=AF.Square,
                                                                        accum_out=stat[0:rn, cc:cc + 1]), ["vn%d" % c], GEK + ["st%d" % cc])
        for c in chunks:
            s0 = sc[c]
            c3 = scol(3)
            op(DVE, lambda e, s0=s0, c3=c3: e.tensor_reduce(out=stat[0:rn, c3:c3 + 1], in_=stat[0:rn, s0:s0 + 4], axis=AX.X, op=ALU.add),
               ["st%d" % (s0 + i) for i in range(4)], ["st%d" % c3])
            r = rstd_from(c3, rn, GH)
            if sample:
                for hf in range(2):
                    vap, vk = vfs2(hf)
                    op(DVE, lambda e, r=r, vap=vap, hf=hf: e.scalar_tensor_tensor(out=vap, in0=vap, scalar=stat[0:rn, r:r + 1],
                                                                                in1=vnw[0:rn, hf * 1024:(hf + 1) * 1024], op0=ALU.mult, op1=ALU.mult),
                       vk + ["st%d" % r, "c_vnw"], vk)
                    out_dmas.append(dma(POOL, gmv_d[:, hf * 1024:(hf + 1) * 1024], vap, vk, []))
                    op(DVE, lambda e, c=c, vap=vap, hf=hf: e.tensor_copy(out=vn[0:rn, c, hf * 1024:(hf + 1) * 1024], in_=vap), vk, ["vn%d" % c])
            else:
                op(DVE, lambda e, r=r, c=c: e.scalar_tensor_tensor(out=vn[0:rn, c, :], in0=vn[0:rn, c, :], scalar=stat[0:rn, r:r + 1],
                                                                  in1=vnw[0:rn, :], op0=ALU.mult, op1=ALU.mult),
                   ["vn%d" % c, "st%d" % r, "c_vnw"], ["vn%d" % c])
        for i in range(4):
            wv, wk = yield dict(a=(REQ(wview(s_w_in, i * 512, 512), KC, 512, SK["in"], f32=wview(w_in_d, i * 512, 512), bid=("in", i * 512)),))
            for j in range(4):
                jj = i * 4 + j
                b = bank()
                for k in range(KC):
                    mm(ps[:, b, 0:ntok], None if wv is None else wv[:, k, j * 128:(j + 1) * 128], hT[:, k, tk0:tk0 + ntok], k == 0, k == KC - 1,
                       hT_keys(ctx) + [wk], ["ps%d" % b])
                op(ACT, lambda e, b=b, jj=jj: e.activation(out=big[:, jj, tk0:tk0 + ntok], in_=ps[:, b, 0:ntok], func=AF.Gelu_apprx_tanh),
                   ["ps%d" % b], ["uT%d_%d" % (jj // 4, c) for c in chunks])
        wsm, wsk = (wsTs, "c_wsTs") if sample else (wsT, "c_wsT")
        bsm, bsk = (bss, "c_bss") if sample else (bsp, "c_bsp")
        for c in chunks:
            for q4 in range(4):
                b = bank()
                for qq in range(4):
                    cc = q4 * 4 + qq
                    g = cc // 2
                    mm(ps[:, b, qq * rn:(qq + 1) * rn], vn[0:rn, c, cc * 128:(cc + 1) * 128], wsm[0:rn, g, 0:rn], True, False,
                       ["vn%d" % c, wsk], ["ps%d" % b])
                    mm(ps[:, b, qq * rn:(qq + 1) * rn], ones[0:1, :], bsm[0:1, g, 0:rn], False, True, ["c_ones", bsk], ["ps%d" % b])
                op(DVE, lambda e, b=b, q4=q4, c=c: e.tensor_tensor(
                    out=big[:, q4 * 4:(q4 + 1) * 4, c * rn:(c + 1) * rn],
                    in0=ps[:, b, 0:4 * rn].rearrange("p (q t) -> p q t", q=4),
                    in1=big[:, q4 * 4:(q4 + 1) * 4, c * rn:(c + 1) * rn], op=ALU.mult),
                   ["ps%d" % b, "uT%d_%d" % (q4, c)], ["uT%d_%d" % (q4, c)])
        for n in range(2):
            banks = {}
            for kh in range(2):
                wv, wk = yield dict(a=(REQ(wview(s_w_out, n * 512, 512)[:, kh * 8:(kh + 1) * 8, :], 8, 512, SK["out"], f32=wview(w_out_d, n * 512, 512)[:, kh * 8:(kh + 1) * 8, :], bid=("out", n * 512 + kh)),))
                for c in chunks:
                    if kh == 0:
                        banks[c] = bank()
                    b = banks[c]
                    for k in range(8):
                        kk = kh * 8 + k
                        mm(ps[0:rn, b, :], big[:, kk, c * rn:(c + 1) * rn], None if wv is None else wv[:, k, :], kk == 0, kk == 15,
                           ["uT%d_%d" % (kk // 4, c), wk], ["ps%d" % b])
                    if kh == 1:
                        op(DVE, lambda e, b=b, c=c, n=n: e.tensor_tensor(out=x[0:rn, XS(c), n * 512:(n + 1) * 512], in0=ps[0:rn, b, :],
                                                                        in1=x[0:rn, XS(c), n * 512:(n + 1) * 512], op=ALU.add),
                           ["ps%d" % b, XK(c)], [XK(c)])

    def ffn(ctx, l):
        rn, chunks, ntok = ctx["rn"], ctx["chunks"], ctx["ntok"]
        tk0 = ctx.get("t0", 0)
        nseq, L = ctx["nseq"], ctx["L"]
        sample = ctx["sample"]
        norm_to_hT(ctx, 1 if l == 0 else 5)
        gk = "gst%d" % l
        ns2 = nseq * 2
        gstx = gsts if sample else gstp
        for i in range(6):
            ncol = 512 if i < 5 else 256
            gv, gkk = yield dict(a=(REQ(wview(s_w_gate[l], i * 512, ncol), KC, ncol, SK["gate%d" % l], live=1, f32=wview(w_gate_d[l], i * 512, ncol), bid=("gate%d" % l, i * 512)),))
            uv, ukk = yield dict(a=(REQ(wview(s_w_up[l], i * 512, ncol), KC, ncol, SK["up%d" % l], live=2, f32=wview(w_up_d[l], i * 512, ncol), bid=("up%d" % l, i * 512)),))
            for j in range(ncol // 128):
                jj = i * 4 + j
                bg = bank()
                for k in range(KC):
                    mm(ps[:, bg, 0:ntok], None if gv is None else gv[:, k, j * 128:(j + 1) * 128], hT[:, k, tk0:tk0 + ntok], k == 0, k == KC - 1,
                       hT_keys(ctx) + [gkk], ["ps%d" % bg])
                bu = bank()
                for k in range(KC):
                    mm(ps[:, bu, 0:ntok], None if uv is None else uv[:, k, j * 128:(j + 1) * 128], hT[:, k, tk0:tk0 + ntok], k == 0, k == KC - 1,
                       hT_keys(ctx) + [ukk], ["ps%d" % bu])
                cbi = jj % 2
                G3 = ps[:, bg, 0:ntok].rearrange("p (s l) -> p s l", s=nseq)
                C3 = cb[:, cbi, 0:ntok].rearrange("p (s l) -> p s l", s=nseq)
                GP3 = gp[:, cbi, 0:nseq * (L + 2)].rearrange("p (s l) -> p s l", s=nseq)
                S3 = gstx[:, l, jj, 0:ns2].rearrange("p (s k) -> p s k", s=nseq)
                w0 = convw[:, l, jj, 0:1]
                w1 = convw[:, l, jj, 1:2]
                w2 = convw[:, l, jj, 2:3]
                bb = convw[:, l, jj, 3:4]
                ck = "cb%d" % cbi
                gpk = "gp%d" % cbi
                op(POOL, lambda e, GP3=GP3, S3=S3: e.tensor_copy(out=GP3[:, :, 0:2], in_=S3), [gk], [gpk])
                op(ACT, lambda e, GP3=GP3, G3=G3: e.activation(out=GP3[:, :, 2:L + 2], in_=G3, func=AF.Copy), ["ps%d" % bg], [gpk])
                op(ACT, lambda e, cbi=cbi, bu=bu: e.activation(out=ub[:, cbi, 0:ntok], in_=ps[:, bu, 0:ntok], func=AF.Copy), ["ps%d" % bu], ["ub%d" % cbi])
                op(DVE, lambda e, GP3=GP3, C3=C3, w2=w2, bb=bb: e.tensor_scalar(out=C3, in0=GP3[:, :, 2:L + 2], scalar1=w2, scalar2=bb,
                                                                             op0=ALU.mult, op1=ALU.add), [gpk, "c_convw"], [ck])
                op(DVE, lambda e, GP3=GP3, C3=C3, w1=w1: e.scalar_tensor_tensor(out=C3, in0=GP3[:, :, 1:L + 1], scalar=w1, in1=C3,
                                                                               op0=ALU.mult, op1=ALU.add), [gpk, ck, "c_convw"], [ck])
                op(DVE, lambda e, GP3=GP3, C3=C3, w0=w0: e.scalar_tensor_tensor(out=C3, in0=GP3[:, :, 0:L], scalar=w0, in1=C3,
                                                                               op0=ALU.mult, op1=ALU.add), [gpk, ck, "c_convw"], [ck])
                if sample:
                    N3 = gnew[:, jj, 0:ns2].rearrange("p (s k) -> p s k", s=nseq)
                    op(POOL, lambda e, GP3=GP3, N3=N3: e.tensor_copy(out=N3, in_=GP3[:, :, L:L + 2]), [gpk], ["gnew"])
                else:
                    op(POOL, lambda e, GP3=GP3, S3=S3: e.tensor_copy(out=S3, in_=GP3[:, :, L:L + 2]), [gpk, gk], [gk])
                op(ACT, lambda e, cbi=cbi: e.activation(out=ge[:, cbi, 0:ntok], in_=cb[:, cbi, 0:ntok], func=AF.Gelu_apprx_tanh),
                   [ck], ["ge%d" % cbi])
                op(DVE, lambda e, cbi=cbi, jj=jj: e.tensor_tensor(out=big[:, jj, tk0:tk0 + ntok], in0=ub[:, cbi, 0:ntok], in1=ge[:, cbi, 0:ntok], op=ALU.mult),
                   ["ub%d" % cbi, "ge%d" % cbi], ["aT%d_%d" % (jj, c) for c in chunks])
        for n in range(2):
            banks = {}
            for kh in range(2):
                wv, wk = yield dict(a=(REQ(wview(s_w_down[l], n * 512, 512)[:, kh * 11:(kh + 1) * 11, :], 11, 512, SK["down%d" % l], f32=wview(w_down_d[l], n * 512, 512)[:, kh * 11:(kh + 1) * 11, :], bid=("down%d" % l, n * 512 + kh)),))
                for c in chunks:
                    if kh == 0:
                        banks[c] = bank()
                    b = banks[c]
                    for k in range(11):
                        kk = kh * 11 + k
                        mm(ps[0:rn, b, :], big[:, kk, c * rn:(c + 1) * rn], None if wv is None else wv[:, k, :], kk == 0, kk == NFC - 1,
                           ["aT%d_%d" % (kk, c), wk], ["ps%d" % b])
                    if kh == 1:
                        op(DVE, lambda e, b=b, c=c, n=n: e.tensor_tensor(out=x[0:rn, XS(c), n * 512:(n + 1) * 512], in0=ps[0:rn, b, :],
                                                                        in1=x[0:rn, XS(c), n * 512:(n + 1) * 512], op=ALU.add),
                           ["ps%d" % b, XK(c)], [XK(c)])

    def ple(ctx, l):
        rn, chunks = ctx["rn"], ctx["chunks"]
        norm_to_hT(ctx, 2 if l == 0 else 6)
        for c in chunks:
            pbuf = c % 2
            dma(SP, pf[0:rn, pbuf, :], ctx["p_src"](l, c), [], ["pf%d" % pbuf])
            op(DVE, lambda e, pbuf=pbuf: e.tensor_copy(out=pb[0:rn, pbuf, :], in_=pf[0:rn, pbuf, :]), ["pf%d" % pbuf], ["pb%d" % pbuf])
            b = bank()
            for k in range(2):
                op(PE, lambda e, k=k, b=b, pbuf=pbuf: e.transpose(psT(b, 1024)[:, k * rn:(k + 1) * rn], pb[0:rn, pbuf, k * 128:(k + 1) * 128],
                                                                  identb[0:rn, 0:rn]), ["pb%d" % pbuf, "c_identb"], ["ps%d" % b])
            op(DVE, lambda e, b=b, c=c: e.tensor_copy(out=pT[:, :, c * rn:(c + 1) * rn], in_=psT(b, 2 * rn).rearrange("p (k t) -> p k t", k=2)),
               ["ps%d" % b], ["pT%d" % c])
        pv, pk_ = yield dict(a=(REQ(wview(s_w_pp[l], 0, 1024), 2, 1024, SK["pp%d" % l], live=1, f32=wview(w_pp_d[l], 0, 1024), bid=("pp%d" % l, 0)),))
        for n in range(2):
            gv, gk_ = yield dict(a=(REQ(wview(s_w_pg[l], n * 512, 512), KC, 512, SK["pg%d" % l], live=n + 2, f32=wview(w_pg_d[l], n * 512, 512), bid=("pg%d" % l, n * 512)),))
            for c in chunks:
                tb = c % 2
                b1 = bank()
                for k in range(KC):
                    mm(ps[0:rn, b1, :], hT[:, k, c * rn:(c + 1) * rn], None if gv is None else gv[:, k, :], k == 0, k == KC - 1,
                       ["hT%d" % c, gk_], ["ps%d" % b1])
                op(ACT, lambda e, b1=b1, tb=tb: e.activation(out=th[0:rn, tb, :], in_=ps[0:rn, b1, :], func=AF.Tanh, scale=0.5),
                   ["ps%d" % b1], ["th%d" % tb])
                b2 = bank()
                for k in range(2):
                    mm(ps[0:rn, b2, :], pT[:, k, c * rn:(c + 1) * rn], None if pv is None else pv[:, k, n * 512:(n + 1) * 512], k == 0, k == 1,
                       ["pT%d" % c, pk_], ["ps%d" % b2])
                op(DVE, lambda e, b2=b2, tb=tb: e.scalar_tensor_tensor(out=th[0:rn, tb, :], in0=th[0:rn, tb, :], scalar=1.0, in1=ps[0:rn, b2, :],
                                                                      op0=ALU.add, op1=ALU.mult), ["ps%d" % b2, "th%d" % tb], ["th%d" % tb])
                op(DVE, lambda e, tb=tb, c=c, n=n: e.scalar_tensor_tensor(out=x[0:rn, XS(c), n * 512:(n + 1) * 512], in0=th[0:rn, tb, :], scalar=0.5,
                                                                         in1=x[0:rn, XS(c), n * 512:(n + 1) * 512], op0=ALU.mult, op1=ALU.add),
                   ["th%d" % tb, XK(c)], [XK(c)])

    def rope(src3, dst3, rn, nh, c, rkeys, wkeys):
        cos = rope_t[0:rn, c, 0:8].unsqueeze(1).to_broadcast([rn, nh, 8])
        sin = rope_t[0:rn, c, 8:16].unsqueeze(1).to_broadcast([rn, nh, 8])
        t = [rt[0:rn, i, 0:nh, :] for i in range(4)]
        tk = ["rt%d" % i for i in range(4)]
        x1 = src3[:, :, 0:8]
        x2 = src3[:, :, 8:16]
        op(DVE, lambda e: e.tensor_tensor(out=t[0], in0=x1, in1=cos, op=ALU.mult), rkeys + ["rope_t"], [tk[0]], n=64)
        op(DVE, lambda e: e.tensor_tensor(out=t[1], in0=x2, in1=sin, op=ALU.mult), rkeys + ["rope_t"], [tk[1]], n=64)
        op(DVE, lambda e: e.tensor_tensor(out=t[2], in0=x2, in1=cos, op=ALU.mult), rkeys + ["rope_t"], [tk[2]], n=64)
        op(DVE, lambda e: e.tensor_tensor(out=t[3], in0=x1, in1=sin, op=ALU.mult), rkeys + ["rope_t"], [tk[3]], n=64)
        op(DVE, lambda e: e.tensor_tensor(out=dst3[:, :, 0:8], in0=t[0], in1=t[1], op=ALU.subtract), [tk[0], tk[1]], wkeys, n=64)
        op(DVE, lambda e: e.tensor_tensor(out=dst3[:, :, 8:16], in0=t[2], in1=t[3], op=ALU.add), [tk[2], tk[3]], wkeys, n=64)

    def shared_kv(ctx):
        rn, chunks = ctx["rn"], ctx["chunks"]
        norm_to_hT(ctx, 3)
        wv, wk = yield dict(a=(REQ(wview(s_w_kv, 0, 256), KC, 256, SK["kv"], f32=wview(w_kv_d, 0, 256), bid=("kv", 0)),))
        for c in chunks:
            kb_ = c % 2
            b = bank()
            for k in range(KC):
                mm(ps[0:rn, b, 0:256], hT[:, k, c * rn:(c + 1) * rn], None if wv is None else wv[:, k, :], k == 0, k == KC - 1,
                   ["hT%d" % c, wk], ["ps%d" % b])
            op(ACT, lambda e, b=b, kb_=kb_: e.activation(out=kvf[0:rn, kb_, :], in_=ps[0:rn, b, 0:256], func=AF.Copy), ["ps%d" % b], ["kvf%d" % kb_])
            k3p = ps[0:rn, b, 0:128].rearrange("p (h d) -> p h d", h=2)
            k3 = kvf[0:rn, kb_, 0:128].rearrange("p (h d) -> p h d", h=2)
            rope(k3, k3, rn, 2, c, ["kvf%d" % kb_], ["kvf%d" % kb_])
            if ctx["sample"]:
                out_dmas.append(dma(POOL, nks_d, kvf[0:rn, kb_, 0:128], ["kvf%d" % kb_], []))
                out_dmas.append(dma(POOL, nvs_d, kvf[0:rn, kb_, 128:256], ["kvf%d" % kb_], []))
                op(DVE, lambda e, kb_=kb_: e.tensor_copy(out=kvb[:, :], in_=kvf[0:rn, kb_, :]), ["kvf%d" % kb_], ["kvb"])
                dma(POOL, s_kv, kvb[:, :], ["kvb"], ["s_kv"])
                dma(POOL, knew4[:, :, :], s_kv.rearrange("(b t) f -> t b f", t=4)[:, :, 128:256], ["s_kv"], ["knew4"])
                bt = bank()
                op(PE, lambda e, bt=bt: e.transpose(psT(bt, 64), kvb[:, 0:128], identb[0:64, 0:64]), ["kvb", "c_identb"], ["ps%d" % bt])
                op(DVE, lambda e, bt=bt: e.tensor_copy(out=kTn[:, :], in_=psT(bt, 64)), ["ps%d" % bt], ["kTnew"])
            else:
                if ctx["last"] and c == chunks[-1]:
                    out_dmas.append(dma(POOL, nkp_d, kvf[0:rn, kb_, 0:128], ["kvf%d" % kb_], []))
                    out_dmas.append(dma(POOL, nvp_d, kvf[0:rn, kb_, 128:256], ["kvf%d" % kb_], []))
                op(DVE, lambda e, kb_=kb_: e.tensor_copy(
                    out=kb2[0:rn, kb_, :].rearrange("p (g u d) -> p g u d", g=2, u=2),
                    in_=kvf[0:rn, kb_, 0:128].rearrange("p (g d) -> p g d", g=2).unsqueeze(2).to_broadcast([rn, 2, 2, 64])),
                   ["kvf%d" % kb_], ["kb2_%d" % kb_], n=256)
                op(ACT, lambda e, kb_=kb_, c=c: e.activation(out=vtok[0:rn, c + 1, :], in_=kvf[0:rn, kb_, 128:256], func=AF.Copy),
                   ["kvf%d" % kb_], ["vt%d" % (c + 1)])
                bt = bank()
                for g in range(2):
                    op(PE, lambda e, g=g, bt=bt, kb_=kb_: e.transpose(psT(bt, 256)[:, g * 128:(g + 1) * 128], kb2[0:rn, kb_, g * 128:(g + 1) * 128],
                                                                      identb[0:rn, 0:rn]), ["kb2_%d" % kb_, "c_identb"], ["ps%d" % bt])
                op(DVE, lambda e, bt=bt, c=c: e.tensor_copy(out=kTd[:, :, (c + 1) * 128:(c + 2) * 128],
                                                            in_=psT(bt, 256).rearrange("p (g t) -> p g t", g=2)), ["ps%d" % bt], ["kT%d" % (c + 1)])

    def q_proj(ctx):
        rn, chunks = ctx["rn"], ctx["chunks"]
        norm_to_hT(ctx, 4)
        qw = []
        for n in range(2):
            r_ = yield dict(a=(REQ(wview(s_w_q, n * 512, 512), KC, 512, SK["q"], live=n + 1, f32=wview(w_q_d, n * 512, 512), bid=("q", n * 512)),))
            qw.append(r_)
        for c in chunks:
            qp = c % 2
            for n in range(2):
                wv, wk = qw[n]
                b = bank()
                for k in range(KC):
                    mm(ps[0:rn, b, :], hT[:, k, c * rn:(c + 1) * rn], None if wv is None else wv[:, k, :], k == 0, k == KC - 1,
                       ["hT%d" % c, wk], ["ps%d" % b])
                op(ACT, lambda e, b=b, qp=qp, n=n: e.activation(out=qb[0:rn, qp, n * 512:(n + 1) * 512], in_=ps[0:rn, b, :], func=AF.Copy),
                   ["ps%d" % b], ["qb%d" % qp])
                q3p = ps[0:rn, b, :].rearrange("p (h d) -> p h d", h=8)
                q3 = qb[0:rn, qp, n * 512:(n + 1) * 512].rearrange("p (h d) -> p h d", h=8)
                rope(q3, q3, rn, 8, c, ["qb%d" % qp], ["qb%d" % qp])
            bt = bank()
            if ctx["sample"]:
                for h in range(16):
                    g, hl = h // 8, h % 8
                    op(PE, lambda e, h=h, g=g, hl=hl, bt=bt, qp=qp: e.transpose(psT(bt, 512)[g * 64:(g + 1) * 64, hl * 64:(hl + 1) * 64],
                                                                              qb[0:rn, qp, h * 64:(h + 1) * 64], identb[0:64, 0:64]),
                       ["qb%d" % qp, "c_identb"], ["ps%d" % bt])
                op(DVE, lambda e: e.memset(qT2[:], 0.0), [], ["qT2"])
                op(DVE, lambda e, bt=bt: e.tensor_copy(out=qtmp[:, :], in_=psT(bt, 512)), ["ps%d" % bt], ["qtmp"])
                for g in range(2):
                    op(DVE, lambda e, g=g: e.tensor_copy(
                        out=qT2[g * 64:(g + 1) * 64, :, g * 32:(g + 1) * 32].rearrange("p b (h t) -> p h b t", h=8),
                        in_=qtmp[g * 64:(g + 1) * 64, :].rearrange("p (h b t) -> p h b t", h=8, b=16)), ["qtmp"], ["qT2"])
            else:
                for k in range(KC):
                    op(PE, lambda e, k=k, bt=bt, qp=qp: e.transpose(psT(bt, 1024)[:, k * 128:(k + 1) * 128], qb[0:rn, qp, k * 128:(k + 1) * 128],
                                                                    identb[:, :]), ["qb%d" % qp, "c_identb"], ["ps%d" % bt])
                op(DVE, lambda e, bt=bt, c=c: e.tensor_copy(out=qT[:, :, c * 128:(c + 1) * 128], in_=psT(bt, 1024).rearrange("p (k t) -> p k t", k=KC)),
                   ["ps%d" % bt], ["qT%d" % c])

    def attn_s1(ctx, c, hg, par):
        mk, mkk = (maskb, "c_maskb") if (ctx["first"] and c == 0) else (maska, "c_maska")
        bs = 2 + 2 * par
        S = ps[:, bs:bs + 2, :].rearrange("p a (h k) -> p (a h) k", h=2)
        skeys = ["ps%d" % bs, "ps%d" % (bs + 1)]
        for hi in range(4):
            h = hg * 4 + hi
            g, kc_, hh = h // 8, h // 2, h % 2
            mm(S[:, hi, :], qT[hh * 64:(hh + 1) * 64, kc_, c * 128:(c + 1) * 128], kTd[hh * 64:(hh + 1) * 64, g, c * 128:c * 128 + 256],
               True, False, ["qT%d" % c, "kT%d" % c, "kT%d" % (c + 1)], [skeys[hi // 2]])
            mm(S[:, hi, :], identb[:, :], mk[:, :], False, True, ["c_identb", mkk], [skeys[hi // 2]])

    def attn_s1b(ctx, c, hg, par):
        bs = 2 + 2 * par
        S = ps[:, bs:bs + 2, :].rearrange("p a (h k) -> p (a h) k", h=2)
        skeys = ["ps%d" % bs, "ps%d" % (bs + 1)]
        A = att[:, par]
        ak = "att%d" % par
        op(DVE, lambda e, S=S, A=A: e.tensor_reduce(out=A[:, 0, :], in_=S, axis=AX.X, op=ALU.max), skeys, [ak])
        op(DVE, lambda e, A=A, hg=hg: e.tensor_tensor(out=A[:, 1, :], in0=A[:, 0, :], in1=sink8[:, hg * 4:(hg + 1) * 4], op=ALU.max),
           [ak, "c_sink8"], [ak], n=4)
        op(DVE, lambda e, A=A: e.tensor_scalar(out=A[:, 2, :], in0=A[:, 1, :], scalar1=-0.125, scalar2=None, op0=ALU.mult), [ak], [ak], n=4)
        for hi in range(4):
            op(ACT, lambda e, S=S, A=A, hi=hi, par=par: e.activation(out=Pm[:, par, hi, :], in_=S[:, hi, :], func=AF.Exp, bias=A[:, 2, hi:hi + 1],
                                                                    scale=0.125, accum_out=A[:, 3, hi:hi + 1]),
               skeys + [ak], ["Pm%d" % par, ak])
        op(DVE, lambda e, A=A, hg=hg: e.scalar_tensor_tensor(out=A[:, 4, :], in0=sink8[:, hg * 4:(hg + 1) * 4], scalar=0.125, in1=A[:, 2, :],
                                                            op0=ALU.mult, op1=ALU.add), [ak, "c_sink8"], [ak])
        op(ACT, lambda e, A=A: e.activation(out=A[:, 4, :], in_=A[:, 4, :], func=AF.Exp), [ak], [ak])
        op(DVE, lambda e, A=A: e.tensor_tensor(out=A[:, 4, :], in0=A[:, 4, :], in1=A[:, 3, :], op=ALU.add), [ak], [ak], n=4)
        op(DVE, lambda e, A=A: e.reciprocal(out=A[:, 5, :], in_=A[:, 4, :]), [ak], [ak], n=4)
        op(DVE, lambda e, A=A, par=par: e.tensor_tensor(out=Dg[:, par], in0=identf[:, :].unsqueeze(1).to_broadcast([128, 4, 128]),
                                                       in1=A[:, 5, :].unsqueeze(2).to_broadcast([128, 4, 128]), op=ALU.mult),
           [ak, "c_identf"], ["Dg%d" % par])

    def attn_s2(ctx, c, hg, par):
        bo = 0
        bp = 6
        PTp = ps[:, bp:bp + 2, :].rearrange("p a (h k) -> p (a h) k", h=2)
        pkeys = ["ps%d" % bp, "ps%d" % (bp + 1)]
        for hi in range(4):
            for blk in range(2):
                mm(PTp[:, hi, blk * 128:(blk + 1) * 128], Pm[:, par, hi, blk * 128:(blk + 1) * 128], Dg[:, par, hi, :], True, True,
                   ["Pm%d" % par, "Dg%d" % par], [pkeys[hi // 2]])
        PT3 = PT[:, par].rearrange("p h b t -> p h (b t)")
        op(ACT, lambda e, PTp=PTp, PT3=PT3: e.activation(out=PT3[:, 0:2, :], in_=PTp[:, 0:2, :], func=AF.Copy), [pkeys[0]], ["PT%d_0" % par])
        op(DVE, lambda e, PTp=PTp, PT3=PT3: e.tensor_copy(out=PT3[:, 2:4, :], in_=PTp[:, 2:4, :]), [pkeys[1]], ["PT%d_1" % par])
        for hi in range(4):
            h = hg * 4 + hi
            g, kc_, hh = h // 8, h // 2, h % 2
            for blk in range(2):
                mm(ps[hh * 64:(hh + 1) * 64, bo + kc_ // 4, (kc_ % 4) * 128:(kc_ % 4 + 1) * 128], vtok[:, c + blk, g * 64:(g + 1) * 64],
                   PT[:, par, hi, blk, :], blk == 0, blk == 1, ["vt%d" % (c + blk), "PT%d_%d" % (par, hi // 2)], ["ps%d" % (bo + kc_ // 4)])

    def attn_layer_prompt(ctx):
        rn, chunks = ctx["rn"], ctx["chunks"]
        yield from q_proj(ctx)
        ow = []
        for n in range(2):
            r_ = yield dict(a=(REQ(wview(s_w_o, n * 512, 512), KC, 512, SK["o"], live=n + 1, f32=wview(w_o_d, n * 512, 512), bid=("o", n * 512)),))
            ow.append(r_)
        groups = [(c, hg) for c in chunks for hg in range(4)]

        def finish_chunk(c):
            op_par = c % 2
            op(DVE, lambda e, op_par=op_par: e.tensor_copy(out=oT[:, op_par], in_=ps[:, 0:2, :].rearrange("p a (k t) -> p (a k) t", k=4)),
               ["ps0", "ps1"], ["oT%d" % op_par])
            for n in range(2):
                wv, wk = ow[n]
                b = 6 + n
                for k in range(KC):
                    mm(ps[0:rn, b, :], oT[:, op_par, k, :], None if wv is None else wv[:, k, :], k == 0, k == KC - 1,
                       ["oT%d" % op_par, wk], ["ps%d" % b])
                op(DVE, lambda e, b=b, c=c, n=n: e.tensor_tensor(out=x[0:rn, XS(c), n * 512:(n + 1) * 512], in0=ps[0:rn, b, :],
                                                                in1=x[0:rn, XS(c), n * 512:(n + 1) * 512], op=ALU.add),
                   ["ps%d" % b, XK(c)], [XK(c)])

        attn_s1(ctx, groups[0][0], groups[0][1], 0)
        attn_s1b(ctx, groups[0][0], groups[0][1], 0)
        for i, (c, hg) in enumerate(groups):
            if i + 1 < len(groups):
                attn_s1(ctx, groups[i + 1][0], groups[i + 1][1], (i + 1) % 2)
            attn_s2(ctx, c, hg, i % 2)
            if i + 1 < len(groups):
                attn_s1b(ctx, groups[i + 1][0], groups[i + 1][1], (i + 1) % 2)
            if hg == 3:
                finish_chunk(c)

    def attn_layer_sample(ctx):
        rn = 64
        yield from q_proj(ctx)
        ow = []
        for n in range(2):
            r_ = yield dict(a=(REQ(wview(s_w_o, n * 512, 512), KC, 512, SK["o"], live=n + 1, f32=wview(w_o_d, n * 512, 512), bid=("o", n * 512)),))
            ow.append(r_)
        for bi in range(16):
            par = bi % 2
            bt = bank()
            op(PE, lambda e, bt=bt, bi=bi: e.transpose(psT(bt, 128), ckb[:, bi, :], identb[:, :]), ["ckb_all", "c_identb"], ["ps%d" % bt])
            op(DVE, lambda e, bt=bt, par=par: e.tensor_copy(out=kTc[:, par, :], in_=psT(bt, 128)), ["ps%d" % bt], ["kTc%d" % par])
            bs = bank()
            mm(ps[0:64, bs, 0:132], identb[0:64, 0:64], masks[:, :], True, False, ["c_identb", "c_masks"], ["ps%d" % bs])
            mm(ps[0:64, bs, 0:128], qT2[:, bi, :], kTc[:, par, :], False, False, ["qT2", "kTc%d" % par], ["ps%d" % bs])
            mm(ps[0:64, bs, 128:132], qT2[:, bi, :], kTn[:, bi * 4:(bi + 1) * 4], False, True, ["qT2", "kTnew"], ["ps%d" % bs])
            A = att[0:64, par]
            ak = "att%d" % par
            S = ps[0:64, bs, 0:132]
            op(DVE, lambda e, S=S, A=A: e.tensor_reduce(out=A[:, 0, 0:1], in_=S, axis=AX.X, op=ALU.max), ["ps%d" % bs], [ak])
            op(DVE, lambda e, A=A: e.tensor_tensor(out=A[:, 1, 0:1], in0=A[:, 0, 0:1], in1=sinkr[:, 1:2], op=ALU.max), [ak, "c_sinkr8"], [ak])
            op(DVE, lambda e, A=A: e.tensor_scalar(out=A[:, 2, 0:1], in0=A[:, 1, 0:1], scalar1=-0.125, scalar2=None, op0=ALU.mult), [ak], [ak], n=4)
            op(ACT, lambda e, S=S, A=A, par=par: e.activation(out=Pm[0:64, par, 0, 0:132], in_=S, func=AF.Exp, bias=A[:, 2, 0:1], scale=0.125,
                                                             accum_out=A[:, 3, 0:1]), ["ps%d" % bs, ak], ["Pm%d" % par, ak])
            op(DVE, lambda e, A=A: e.tensor_tensor(out=A[:, 4, 0:1], in0=sinkr[:, 0:1], in1=A[:, 2, 0:1], op=ALU.add), [ak, "c_sinkr"], [ak])
            op(ACT, lambda e, A=A: e.activation(out=A[:, 4, 0:1], in_=A[:, 4, 0:1], func=AF.Exp), [ak], [ak])
            op(DVE, lambda e, A=A: e.tensor_tensor(out=A[:, 4, 0:1], in0=A[:, 4, 0:1], in1=A[:, 3, 0:1], op=ALU.add), [ak], [ak], n=4)
            op(DVE, lambda e, A=A: e.reciprocal(out=A[:, 5, 0:1], in_=A[:, 4, 0:1]), [ak], [ak])
            op(DVE, lambda e, A=A, par=par: e.tensor_scalar(out=Dg[0:64, par, 0, 0:64], in0=identf[0:64, 0:64], scalar1=A[:, 5, 0:1], scalar2=None,
                                                           op0=ALU.mult), [ak, "c_identf"], ["Dg%d" % par])
            bp = bank()
            mm(ps[:, bp, 0:64], Pm[0:64, par, 0, 0:128], Dg[0:64, par, 0, 0:64], True, True, ["Pm%d" % par, "Dg%d" % par], ["ps%d" % bp])
            mm(ps[0:4, bp, 64:128], Pm[0:64, par, 0, 128:132], Dg[0:64, par, 0, 0:64], True, True, ["Pm%d" % par, "Dg%d" % par], ["ps%d" % bp])
            op(ACT, lambda e, bp=bp, par=par: e.activation(out=PTs[:, par, :], in_=ps[:, bp, 0:64], func=AF.Copy), ["ps%d" % bp], ["PTs%d" % par])
            op(ACT, lambda e, bp=bp, par=par: e.activation(out=PTn[:, par, :], in_=ps[0:4, bp, 64:128], func=AF.Copy), ["ps%d" % bp], ["PTn%d" % par])
            bo = bank()
            mm(ps[0:64, bo, 0:128], PTs[:, par, :], cvb[:, bi, :], True, False, ["PTs%d" % par, "cvb_all"], ["ps%d" % bo])
            mm(ps[0:64, bo, 0:128], PTn[:, par, :], knew4[0:4, bi, :], False, True, ["PTn%d" % par, "knew4"], ["ps%d" % bo])
            op(DVE, lambda e, bo=bo, bi=bi: e.tensor_copy(out=o_all[:, bi, :], in_=ps[0:64, bo, 0:128]), ["ps%d" % bo], ["o_all"])
        dma(POOL, s_o.rearrange("b r d -> r b d"), o_all[:, :, :], ["o_all"], ["s_o"])
        for bi in range(16):
            for g in range(2):
                dma(POOL, o_tok[bi * 4:(bi + 1) * 4, g * 512:(g + 1) * 512].rearrange("t (h d) -> t h d", h=8),
                    s_o[bi, g * 32:(g + 1) * 32, g * 64:(g + 1) * 64].rearrange("(h t) d -> t h d", t=4), ["s_o"], ["o_tok_%d_%d" % (bi, g)])
        if dbg == "pm":
            out_dmas.append(dma(POOL, ys_d[:, 0:256], Pm[0:64, 1, 0, :], ["Pm1"], []))
            out_dmas.append(dma(POOL, ys_d[:, 256:384], o_all[:, 15, :], ["o_all"], []))
            out_dmas.append(dma(POOL, ys_d[:, 384:512], o_all[:, 0, :], ["o_all"], []))
            out_dmas.append(dma(POOL, ys_d[:, 512:536], att[0:64, 1].rearrange("p a b -> p (a b)"), ["att1"], []))
        OTK = ["o_tok_%d_%d" % (bi, g) for bi in range(16) for g in range(2)]
        if dbg == "otok":
            out_dmas.append(dma(POOL, ys_d, o_tok[:, :], OTK, []))
        if dbg == "qb":
            out_dmas.append(dma(POOL, ys_d, qb[0:64, 0, :], ["qb0"], []))
        bt = bank()
        for k in range(KC):
            op(PE, lambda e, k=k, bt=bt: e.transpose(psT(bt, 512)[:, k * 64:(k + 1) * 64], o_tok[:, k * 128:(k + 1) * 128], identb[0:64, 0:64]),
               OTK + ["c_identb"], ["ps%d" % bt])
        op(DVE, lambda e, bt=bt: e.tensor_copy(out=oT[:, 0, :, 0:64], in_=psT(bt, 512).rearrange("p (k t) -> p k t", k=KC)), ["ps%d" % bt], ["oT0"])
        for n in range(2):
            wv, wk = ow[n]
            b = bank()
            for k in range(KC):
                mm(ps[0:rn, b, :], oT[:, 0, k, 0:64], None if wv is None else wv[:, k, :], k == 0, k == KC - 1, ["oT0", wk], ["ps%d" % b])
            op(DVE, lambda e, b=b, n=n: e.tensor_tensor(out=x[0:rn, 0, n * 512:(n + 1) * 512], in0=ps[0:rn, b, :], in1=x[0:rn, 0, n * 512:(n + 1) * 512],
                                                       op=ALU.add), ["ps%d" % b, "x0"], ["x0"])

    def final_norm(ctx):
        rn, chunks = ctx["rn"], ctx["chunks"]
        for c in chunks:
            r = rms_stats(x[0:rn, XS(c), :], rn, D, [XK(c)])
            op(DVE, lambda e, c=c, r=r: e.scalar_tensor_tensor(out=yout[0:rn, 0, :], in0=x[0:rn, XS(c), :], scalar=stat[0:rn, r:r + 1],
                                                               in1=fnw[0:rn, :], op0=ALU.mult, op1=ALU.mult),
               [XK(c), "st%d" % r, "c_fnw"], TH)
            dst = ctx["y_dst"](c)
            if dbg is not None and ctx["sample"]:
                dst = None
            if dst is not None:
                out_dmas.append(dma(POOL, dst, yout[0:rn, 0, :], TH, []))

    def emit_conv_state(state_ap_fn, nrows, dst, key):
        for g6 in range(6):
            n4 = 4 if g6 < 5 else 2
            b = bank()
            for q in range(n4):
                jj = g6 * 4 + q
                op(PE, lambda e, b=b, q=q, jj=jj: e.transpose(ps[0:nrows, b, q * 128:(q + 1) * 128], state_ap_fn(jj), identf[:, :]),
                   [key, "c_identf"], ["ps%d" % b])
            op(DVE, lambda e, b=b, g6=g6, n4=n4: e.tensor_copy(out=tr_out[0:nrows, g6 * 512:g6 * 512 + n4 * 128], in_=ps[0:nrows, b, 0:n4 * 128]),
               ["ps%d" % b], K_F)
        out_dmas.append(dma(POOL, dst, tr_out[0:nrows, :], K_F, []))

    def prompt_ctx(t):
        ctx = dict(rn=128, chunks=CH4, ntok=512, nseq=1, L=512, sample=False, first=(t == 1), last=(t == NT - 1))
        ctx["p_src"] = lambda l, c, t=t: pp_d[l, (t * 4 + c) * 128:(t * 4 + c + 1) * 128, :]
        ctx["y_dst"] = (lambda c, t=t: yp_d[((t - 1) * 4 + c) * 128:((t - 1) * 4 + c + 1) * 128, :]) if t >= 1 else (lambda c: None)
        ctx1 = ctx
        if t == 0:
            ctx = dict(ctx, chunks=[1, 2, 3], ntok=384, L=384, t0=128)
            ctx1 = dict(ctx, chunks=[3], ntok=128, L=128, t0=384)
        return ctx, ctx1

    def prompt_loads(t, ctx):
        cur_tile[0] = t
        for c in ctx["chunks"]:
            if c == 0 and t > 0:
                continue
            dma(SP, x[:, XS(c), :], xp_d[(t * 4 + c) * 128:(t * 4 + c + 1) * 128, :], [], [XK(c)])
        c_lo = ctx["chunks"][0]
        dma(SP, rope_t[:, c_lo:4, :], ropep_d[t * 512 + c_lo * 128:(t + 1) * 512, :].rearrange("(c p) f -> p c f", p=128), [], ["rope_t"])

    def carry_kv():
        op(DVE, lambda e: e.tensor_copy(out=kTd[:, :, 0:128], in_=kTd[:, :, 512:640]), ["kT4"], ["kT0"])
        op(DVE, lambda e: e.tensor_copy(out=vtok[:, 0, :], in_=vtok[:, 4, :]), ["vt4"], ["vt0"])

    def prefetch_next_x(t):
        if t + 1 < n_tiles:
            ns_ = (4 * (t + 1)) % 5
            dma(SP, x[:, ns_, :], xp_d[((t + 1) * 4) * 128:((t + 1) * 4 + 1) * 128, :], [], ["x%d" % ns_])

    def sample_ctx():
        ctx = dict(rn=64, chunks=[0], ntok=64, nseq=16, L=4, sample=True, first=False, last=False)
        ctx["p_src"] = lambda l, c: pss_d[l]
        ctx["y_dst"] = lambda c: ys_d
        return ctx

    def sample_loads():
        dma(SP, x[0:64, 0, :], xs_d, [], ["x0"])
        dma(SP, rope_t[0:64, 0, :], ropes_d, [], ["rope_t"])
        for l in range(2):
            dma(SP, cstf, cst_d[l], [], K_F)
            for g6 in range(6):
                n4 = 4 if g6 < 5 else 2
                b = bank()
                for q in range(n4):
                    jj = g6 * 4 + q
                    op(PE, lambda e, b=b, q=q, jj=jj: e.transpose(ps[:, b, q * 32:(q + 1) * 32], cstf[:, jj * 128:(jj + 1) * 128], identf[0:32, 0:32]),
                       K_F + ["c_identf"], ["ps%d" % b])
                op(DVE, lambda e, b=b, g6=g6, n4=n4, l=l: e.tensor_copy(out=gsts[:, l, g6 * 4:g6 * 4 + n4, :],
                                                                      in_=ps[:, b, 0:n4 * 32].rearrange("p (q r) -> p q r", q=n4)),
                   ["ps%d" % b], ["gst%d" % l])

    def run_all():
        ctxH, ctxH1 = prompt_ctx(0)
        ctxS = sample_ctx()
        if do_sample:
            dma(POOL, ckb[:, :, :], ck_d.rearrange("b k f -> k b f"), [], ["ckb_all"])
            dma(POOL, cvb[:, :, :], cv_d.rearrange("b k f -> k b f"), [], ["cvb_all"])
        prompt_loads(0, ctxH)
        if do_sample:
            sample_loads()
        S = do_sample
        fence(K_A + K_F, K_U)
        fence(K_Q, K_VN)
        drive([gmlp(ctxH)] + ([gmlp(ctxS)] if S else []))
        fence(K_U, K_A)
        drive([ffn(ctxH, 0)] + ([ffn(ctxS, 0)] if S else []))
        if S:
            fence(K_A, K_F)
            emit_conv_state(lambda jj: gnew[:, jj, :], 32, ncs_d[0], "gnew")
        drive([ple(ctxH, 0)] + ([ple(ctxS, 0)] if S else []))
        carry_kv()
        drive([shared_kv(ctxH)] + ([shared_kv(ctxS)] if S else []))
        prefetch_next_x(0)
        fence(K_VN, K_Q)
        drive([attn_layer_prompt(ctxH1)] + ([attn_layer_sample(ctxS)] if S else []))
        fence(K_F, K_A)
        drive([ffn(ctxH1, 1)] + ([ffn(ctxS, 1)] if S else []))
        if S:
            fence(K_A, K_F)
            emit_conv_state(lambda jj: gnew[:, jj, :], 32, ncs_d[1], "gnew")
            drive([ple(ctxS, 1)])
            final_norm(ctxS)
        for t in range(1, n_tiles):
            ctx, ctx1 = prompt_ctx(t)
            prompt_loads(t, ctx)
            fence(K_A + K_F, K_U)
            fence(K_Q, K_VN)
            drive([gmlp(ctx)])
            fence(K_U, K_A)
            drive([ffn(ctx, 0)])
            drive([ple(ctx, 0)])
            carry_kv()
            drive([shared_kv(ctx)])
            prefetch_next_x(t)
            fence(K_VN, K_Q)
            drive([attn_layer_prompt(ctx1)])
            drive([ffn(ctx1, 1)])
            drive([ple(ctx1, 1)])
            final_norm(ctx1)
        fence(K_A, K_F)
        for l in range(2):
            emit_conv_state(lambda jj, l=l: gstp[:, l, jj, 0:2], 2, ncp_d[l], "gst%d" % l)

    run_all()
    ring.planning = False
    bank_ctr[0] = 0
    stat_ctr[0] = 0
    run_all()
    finals = [o for o in out_dmas if o is not None]
    P.emit({POOL: finals})
    st.close()
    return nc


_CACHE = {}


def _host_inputs(inp):
    f32 = np.float32
    x_prompt = np.asarray(inp["x_prompt"], f32)
    p_prompt = np.asarray(inp["p_prompt"], f32)
    x_sample = np.asarray(inp["x_sample"], f32)
    p_sample = np.asarray(inp["p_sample"], f32)
    state = np.asarray(inp["state_ffn_conv"], f32)
    ck = np.asarray(inp["cache_k_win"], f32)
    cv = np.asarray(inp["cache_v_win"], f32)
    norms = [inp["norm_mix"][0], inp["norm_ffn"][0], inp["ple_norm"][0], inp["kv_norm"], inp["norm_mix"][1], inp["norm_ffn"][1],
             inp["ple_norm"][1], inp["final_norm"]]
    nw_fm = np.stack([np.asarray(w, f32).reshape(KC, 128).T for w in norms], axis=1)
    convw = np.zeros((128, 2, NFC, 4), f32)
    cw = np.asarray(inp["ffn_conv_w"], f32)
    cbias = np.asarray(inp["ffn_conv_b"], f32)
    for l in range(2):
        for k in range(3):
            convw[:, l, :, k] = cw[l, k].reshape(NFC, 128).T
        convw[:, l, :, 3] = cbias[l].reshape(NFC, 128).T
    ws = np.asarray(inp["gm_w_s"], f32)[0]
    bs = np.asarray(inp["gm_b_s"], f32)[0]
    tri = np.tril(np.ones((128, 128), bool))
    wsT = np.where(tri[None], ws, 0).transpose(2, 0, 1).copy()
    wsTs = np.zeros((64, 8, 64), f32)
    tri4 = np.tril(np.ones((4, 4), bool))
    blk = np.where(tri4[None], ws[:, :4, :4], 0).transpose(2, 0, 1)
    for b in range(16):
        wsTs[b * 4:(b + 1) * 4, :, b * 4:(b + 1) * 4] = blk
    bsp = bs[None, :, :].copy()
    bss = np.tile(bs[:, :4], (1, 16))[None].copy()
    sinks = np.asarray(inp["attn_sinks"], f32).reshape(1, 16)
    sinkrow = np.repeat(sinks[0], 4).reshape(64, 1).copy()
    half = 8
    inv_freq = (500000.0 ** (-np.arange(half, dtype=np.float32) / half)).astype(f32)
    tq = np.arange(128)[:, None]
    tk = np.arange(128)[None, :]
    mask_a = np.full((128, 256), NEG, f32)
    mask_a[:, 0:128][tk > tq] = 0.0
    mask_a[:, 128:256][tk <= tq] = 0.0
    mask_first = mask_a.copy()
    mask_first[:, 0:128] = NEG
    mask_s = np.full((64, 132), NEG, f32)
    for r in range(64):
        qi = r % 4
        sj = np.arange(132)
        mask_s[r, (sj > qi) & (sj <= qi + 128)] = 0.0
    ident = np.eye(128, dtype=f32)
    common = {
        "gm_w_in": np.ascontiguousarray(np.asarray(inp["gm_w_in"], f32)[0]),
        "gm_w_out": np.ascontiguousarray(np.asarray(inp["gm_w_out"], f32)[0]),
        "w_kv": np.asarray(inp["w_kv"], f32),
        "w_q": np.ascontiguousarray(np.asarray(inp["w_q"], f32)[0]),
        "w_o": np.ascontiguousarray(np.asarray(inp["w_o"], f32)[0]),
        "ffn_w_gate": np.asarray(inp["ffn_w_gate"], f32),
        "ffn_w_up": np.asarray(inp["ffn_w_up"], f32),
        "ffn_w_down": np.asarray(inp["ffn_w_down"], f32),
        "ple_w_gate": np.asarray(inp["ple_w_gate"], f32),
        "ple_w_proj": np.asarray(inp["ple_w_proj"], f32),
        "nw_fm": np.ascontiguousarray(nw_fm), "fnw": np.asarray(inp["final_norm"], f32).reshape(1, D),
        "vnw": np.asarray(inp["gm_v_norm"], f32).reshape(1, GH), "convw": convw,
        "wsT": wsT.astype(f32), "wsTs": wsTs, "bsp": bsp, "bss": bss, "sinks": sinks, "sinkrow": sinkrow,
        "ident": ident, "mask_a": mask_a, "mask_s": mask_s,
    }
    pos_s = (16384 + np.arange(4)).astype(f32)
    ang_s = pos_s[:, None] * inv_freq[None, :]
    rope_s = np.tile(np.concatenate([np.cos(ang_s), np.sin(ang_s)], axis=1).astype(f32), (16, 1))
    maps = []
    for core in range(8):
        b, hf = core // 2, core % 2
        lo = hf * 4096 - 512
        xp = np.zeros((NCHUNK_CORE * 128, D), f32)
        pp = np.zeros((2, NCHUNK_CORE * 128, PLE), f32)
        if hf == 0:
            xp[512:] = x_prompt[b, 0:4096]
            pp[:, 512:] = p_prompt[:, b, 0:4096]
        else:
            xp[:] = x_prompt[b, lo:lo + 4608]
            pp[:] = p_prompt[:, b, lo:lo + 4608]
        pos = (lo + np.arange(NCHUNK_CORE * 128)).astype(f32)
        ang = pos[:, None] * inv_freq[None, :]
        rope_p = np.concatenate([np.cos(ang), np.sin(ang)], axis=1).astype(f32)
        m = dict(common)
        m.update({
            "xp": xp, "pp": pp,
            "xs": np.ascontiguousarray(x_sample[core * 16:(core + 1) * 16].reshape(64, D)),
            "pss": np.ascontiguousarray(p_sample[:, core * 16:(core + 1) * 16].reshape(2, 64, PLE)),
            "cst": np.ascontiguousarray(state[:, core * 16:(core + 1) * 16].reshape(2, 32, FF)),
            "ck": np.ascontiguousarray(ck[core * 16:(core + 1) * 16].reshape(16, 128, 128)),
            "cv": np.ascontiguousarray(cv[core * 16:(core + 1) * 16].reshape(16, 128, 128)),
            "rope_p": rope_p, "rope_s": rope_s,
            "mask_b": mask_first if hf == 0 else mask_a,
        })
        maps.append(m)
    return maps


def kernel(**inputs):
    if "nc" not in _CACHE:
        _CACHE["nc"] = build_program()
    nc = _CACHE["nc"]
    maps = _host_inputs(inputs)
    res = run_bass_kernel_spmd(nc, maps, core_ids=list(range(8)))
    R = res.results
    f32 = np.float32
    y_prompt = np.zeros((4, 8192, D), f32)
    y_sample = np.zeros((128, 4, D), f32)
    gmv = np.zeros((1, 128, 4, GH), f32)
    ncp = np.zeros((2, 4, 2, FF), f32)
    ncs = np.zeros((2, 128, 2, FF), f32)
    nkp = np.zeros((4, 128, 2, 64), f32)
    nvp = np.zeros((4, 128, 2, 64), f32)
    nks = np.zeros((128, 4, 2, 64), f32)
    nvs = np.zeros((128, 4, 2, 64), f32)
    for core in range(8):
        b, hf = core // 2, core % 2
        r = R[core]
        y_prompt[b, hf * 4096:(hf + 1) * 4096] = r["yp"]
        sl = slice(core * 16, (core + 1) * 16)
        y_sample[sl] = r["ys"].reshape(16, 4, D)
        gmv[0, sl] = r["gmv"].reshape(16, 4, GH)
        ncs[:, sl] = r["ncs"].reshape(2, 16, 2, FF)
        nks[sl] = r["nks"].reshape(16, 4, 2, 64)
        nvs[sl] = r["nvs"].reshape(16, 4, 2, 64)
        if hf == 1:
            ncp[:, b] = r["ncp"]
            nkp[b] = r["nkp"].reshape(128, 2, 64)
            nvp[b] = r["nvp"].reshape(128, 2, 64)
    return (y_prompt, y_sample, gmv, ncp, ncs, nkp, nvp, nks, nvs)
```
